# Optimizing a Trainium2 kernel written in Bass

```python
import math
import jax, jax.numpy as jnp
from jax import lax
import numpy as np

D_MODEL = 2048
BATCH = 2
SEQ = 16384
DEPTH = 2
DEC_BATCH = 2
DEC_SEQ = 8192
PAST_LEN = 128

GRID_W = 64
HEAD_DIM = 64
Q_BLOCK = 128
EPS = 1e-6
NEG_INF = -1e30
SCALE = HEAD_DIM ** -0.5
GROUP_WIDTH = D_MODEL // 4
MIX_WIDTH = 4 * GROUP_WIDTH
A_HEADS = GROUP_WIDTH // HEAD_DIM
A_KV_HEADS = A_HEADS // 4
ROPE_THETA = 10000.0
B_QK_DIM = HEAD_DIM
B_V_DIM = 2 * B_QK_DIM
B_HEADS = GROUP_WIDTH // B_V_DIM
C_HEADS = GROUP_WIDTH // HEAD_DIM
C_WIN_ROWS = 8
C_WIN_COLS = 16
D_HEADS = GROUP_WIDTH // HEAD_DIM
D_BRANCHES = ((128, 1), (512, 4), (2048, 16))
T5_BUCKETS = 32
T5_MAX_DIST = 1024
T5_HEADS = B_HEADS + D_HEADS
D_FF = 5632

IN_SPLITS = (A_HEADS * HEAD_DIM, A_KV_HEADS * HEAD_DIM, A_KV_HEADS * HEAD_DIM,
             B_HEADS * 2 * B_QK_DIM, B_HEADS * 2 * B_QK_DIM, B_HEADS * B_V_DIM,
             C_HEADS * HEAD_DIM, C_HEADS * HEAD_DIM, C_HEADS * HEAD_DIM,
             D_HEADS * HEAD_DIM, D_HEADS * HEAD_DIM, D_HEADS * HEAD_DIM)
IN_WIDTH = sum(IN_SPLITS)
IN_SPLIT_POINTS = tuple(int(v) for v in np.cumsum(IN_SPLITS)[:-1])

kernel_name = 'hymba_style_hybrid_bidirectional_encoder'


def rms_norm(x, g):
    xf = x.astype(jnp.float32)
    y = xf * lax.rsqrt(jnp.mean(xf * xf, axis=-1, keepdims=True) + EPS)
    return (y * g.astype(jnp.float32)).astype(x.dtype)


def swiglu(h, w_gate, w_up, w_down):
    return (jax.nn.silu(h @ w_gate) * (h @ w_up)) @ w_down


def to_blocks(x):
    b, t = x.shape[:2]
    return jnp.swapaxes(x.reshape((b, t // Q_BLOCK, Q_BLOCK) + x.shape[2:]), 0, 1)


def from_blocks(y):
    nb, b, qb = y.shape[:3]
    return jnp.swapaxes(y, 0, 1).reshape((b, nb * qb) + y.shape[3:])


def t5_bucket(rel):
    nb = T5_BUCKETS // 2
    max_exact = nb // 2
    side = (rel > 0).astype(jnp.int32) * nb
    n = jnp.abs(rel)
    large = max_exact + (jnp.log(jnp.maximum(n, 1).astype(jnp.float32) / max_exact)
                         / math.log(T5_MAX_DIST / max_exact) * (nb - max_exact)).astype(jnp.int32)
    large = jnp.minimum(large, nb - 1)
    return side + jnp.where(n < max_exact, n, large)


def axial_rope(x):
    t_len = x.shape[1]
    t = jnp.arange(t_len, dtype=jnp.int32)
    n_freq = HEAD_DIM // 4
    inv_freq = ROPE_THETA ** (-jnp.arange(n_freq, dtype=jnp.float32) / n_freq)
    ang = jnp.concatenate([(t // GRID_W).astype(jnp.float32)[:, None] * inv_freq[None, :],
                           (t % GRID_W).astype(jnp.float32)[:, None] * inv_freq[None, :]], axis=-1)
    cos = jnp.cos(ang)[None, :, None, :]
    sin = jnp.sin(ang)[None, :, None, :]
    xf = x.astype(jnp.float32).reshape(x.shape[:-1] + (HEAD_DIM // 2, 2))
    x1, x2 = xf[..., 0], xf[..., 1]
    out = jnp.stack([x1 * cos - x2 * sin, x1 * sin + x2 * cos], axis=-1).reshape(x.shape)
    return out.astype(x.dtype)


def gqa_axial_attention(q, k, v, g_q, g_k):
    b, t_len = q.shape[:2]
    q = axial_rope(rms_norm(q, g_q))
    k = axial_rope(rms_norm(k, g_k))
    qg = q.reshape(b, t_len, A_KV_HEADS, A_HEADS // A_KV_HEADS, HEAD_DIM)

    def block(qi):
        s = jnp.einsum('bqkgd,bskd->bkgqs', qi, k, preferred_element_type=jnp.float32) * SCALE
        p = jax.nn.softmax(s, axis=-1).astype(v.dtype)
        return jnp.einsum('bkgqs,bskd->bqkgd', p, v)

    o = from_blocks(lax.map(block, to_blocks(qg)))
    return o.reshape(b, t_len, A_HEADS * HEAD_DIM)


def differential_attention(q, k, v, lam_params, subln_g, lam_init, t5_b):
    b, t_len = q.shape[:2]
    lp = lam_params.astype(jnp.float32)
    lam = jnp.exp(jnp.sum(lp[0] * lp[1])) - jnp.exp(jnp.sum(lp[2] * lp[3])) + lam_init
    key_pos = jnp.arange(t_len, dtype=jnp.int32)
    starts = jnp.arange(t_len // Q_BLOCK, dtype=jnp.int32) * Q_BLOCK

    def block(args):
        qi, q0 = args
        rel = key_pos[None, :] - (q0 + jnp.arange(Q_BLOCK, dtype=jnp.int32))[:, None]
        bias = jnp.moveaxis(t5_b[t5_bucket(rel)].astype(jnp.float32), -1, 0)
        s = jnp.einsum('bqhmd,bshmd->bhmqs', qi, k, preferred_element_type=jnp.float32) * SCALE
        p = jax.nn.softmax(s + bias[None, :, None], axis=-1)
        a = (p[:, :, 0] - lam * p[:, :, 1]).astype(v.dtype)
        return jnp.einsum('bhqs,bshe->bqhe', a, v)

    o = from_blocks(lax.map(block, (to_blocks(q), starts)))
    o = rms_norm(o, subln_g) * (1.0 - lam_init)
    return o.reshape(b, t_len, B_HEADS * B_V_DIM)


def neighbourhood_attention(q, k, v, rpb):
    b, t_len = q.shape[:2]
    rows = t_len // GRID_W
    kr = min(C_WIN_ROWS, rows)
    n_keys = kr * C_WIN_COLS
    nb = t_len // Q_BLOCK
    t = jnp.arange(t_len, dtype=jnp.int32)
    r, c = t // GRID_W, t % GRID_W
    rs = jnp.clip(r - kr // 2, 0, rows - kr)
    cs = jnp.clip(c - C_WIN_COLS // 2, 0, GRID_W - C_WIN_COLS)
    key_r = jnp.broadcast_to(rs[:, None, None] + jnp.arange(kr, dtype=jnp.int32)[None, :, None], (t_len, kr, C_WIN_COLS))
    key_c = jnp.broadcast_to(cs[:, None, None] + jnp.arange(C_WIN_COLS, dtype=jnp.int32)[None, None, :], (t_len, kr, C_WIN_COLS))
    idx = (key_r * GRID_W + key_c).reshape(nb, Q_BLOCK, n_keys)
    rel_r = (key_r - r[:, None, None] + (C_WIN_ROWS - 1)).reshape(nb, Q_BLOCK, n_keys)
    rel_c = (key_c - c[:, None, None] + (C_WIN_COLS - 1)).reshape(nb, Q_BLOCK, n_keys)

    def block(args):
        qi, idx_b, rr, cc = args
        kg = jnp.take(k, idx_b, axis=1)
        vg = jnp.take(v, idx_b, axis=1)
        s = jnp.einsum('bqhd,bqkhd->bhqk', qi, kg, preferred_element_type=jnp.float32) * SCALE
        s = s + rpb[:, rr, cc].astype(jnp.float32)[None]
        p = jax.nn.softmax(s, axis=-1).astype(v.dtype)
        return jnp.einsum('bhqk,bqkhd->bqhd', p, vg)

    o = from_blocks(lax.map(block, (to_blocks(q), idx, rel_r, rel_c)))
    return o.reshape(b, t_len, C_HEADS * HEAD_DIM)


def dilated_mixture_attention(q, k, v, t5_d):
    b, t_len = q.shape[:2]
    starts = jnp.arange(t_len // Q_BLOCK, dtype=jnp.int32) * Q_BLOCK
    branch_offs = [jnp.arange(-(w // (2 * d)), w // (2 * d) + 1, dtype=jnp.int32) * d for w, d in D_BRANCHES]
    branch_bias = [jnp.transpose(t5_d[t5_bucket(o)]).astype(jnp.float32) for o in branch_offs]

    def block(args):
        qi, q0 = args
        tq = q0 + jnp.arange(Q_BLOCK, dtype=jnp.int32)
        outs, lses = [], []
        for offs, bias in zip(branch_offs, branch_bias):
            pos = tq[:, None] + offs[None, :]
            valid = (pos >= 0) & (pos < t_len)
            pos = jnp.clip(pos, 0, t_len - 1)
            kg = jnp.take(k, pos, axis=1)
            vg = jnp.take(v, pos, axis=1)
            s = jnp.einsum('bqhd,bqkhd->bhqk', qi, kg, preferred_element_type=jnp.float32) * SCALE
            s = jnp.where(valid[None, None], s + bias[None, :, None, :], NEG_INF)
            lse = jax.nn.logsumexp(s, axis=-1)
            p = jnp.exp(s - lse[..., None]).astype(v.dtype)
            outs.append(jnp.einsum('bhqk,bqkhd->bqhd', p, vg))
            lses.append(lse)
        wts = jax.nn.softmax(jnp.stack(lses, axis=0), axis=0).astype(v.dtype)
        return jnp.einsum('gbhq,gbqhd->bqhd', wts, jnp.stack(outs, axis=0))

    o = from_blocks(lax.map(block, (to_blocks(q), starts)))
    return o.reshape(b, t_len, D_HEADS * HEAD_DIM)


def mixing_sublayer(h, layer, w_in, a_q_norm, a_k_norm, b_lambda, b_subln, c_rpb, w_out, t5_table):
    b, t_len, _ = h.shape
    qa, ka, va, qb, kb, vb, qc, kc, vc, qd, kd, vd = jnp.split(h @ w_in, IN_SPLIT_POINTS, axis=-1)
    hd4 = lambda z, nh: z.reshape(b, t_len, nh, HEAD_DIM)
    out_a = gqa_axial_attention(hd4(qa, A_HEADS), hd4(ka, A_KV_HEADS), hd4(va, A_KV_HEADS), a_q_norm, a_k_norm)
    lam_init = 0.8 - 0.6 * math.exp(-0.3 * layer)
    out_b = differential_attention(qb.reshape(b, t_len, B_HEADS, 2, B_QK_DIM),
                                   kb.reshape(b, t_len, B_HEADS, 2, B_QK_DIM),
                                   vb.reshape(b, t_len, B_HEADS, B_V_DIM),
                                   b_lambda, b_subln, lam_init, t5_table[:, :B_HEADS])
    out_c = neighbourhood_attention(hd4(qc, C_HEADS), hd4(kc, C_HEADS), hd4(vc, C_HEADS), c_rpb)
    out_d = dilated_mixture_attention(hd4(qd, D_HEADS), hd4(kd, D_HEADS), hd4(vd, D_HEADS), t5_table[:, B_HEADS:])
    return jnp.concatenate([out_a, out_b, out_c, out_d], axis=-1) @ w_out


def encoder_trunk(x, ffn1_norm, ffn1_w_gate, ffn1_w_up, ffn1_w_down, mix_norm, w_in, a_q_norm, a_k_norm,
                  b_lambda, b_subln, c_rpb, w_out, ffn2_norm, ffn2_w_gate, ffn2_w_up, ffn2_w_down,
                  t5_table, final_norm):
    for l in range(DEPTH):
        x = x + 0.5 * swiglu(rms_norm(x, ffn1_norm[l]), ffn1_w_gate[l], ffn1_w_up[l], ffn1_w_down[l])
        x = x + mixing_sublayer(rms_norm(x, mix_norm[l]), l, w_in[l], a_q_norm[l], a_k_norm[l],
                                b_lambda[l], b_subln[l], c_rpb[l], w_out[l], t5_table)
        x = x + 0.5 * swiglu(rms_norm(x, ffn2_norm[l]), ffn2_w_gate[l], ffn2_w_up[l], ffn2_w_down[l])
    return rms_norm(x, final_norm)


def setup_inputs(seed: int = 0) -> dict:
    key = jax.random.key(seed)
    ks = jax.random.split(key, 24)
    f32 = jnp.float32

    def nrm(k, shape, scale):
        return scale * jax.random.normal(k, shape, dtype=f32)

    def gain(k, shape):
        return 1.0 + 0.02 * jax.random.normal(k, shape, dtype=f32)

    return {
        'x_prompt': nrm(ks[0], (BATCH, SEQ, D_MODEL), 1.0),
        'x_sample': nrm(ks[1], (DEC_BATCH, DEC_SEQ, D_MODEL), 1.0),
        'ffn1_norm': gain(ks[2], (DEPTH, D_MODEL)),
        'ffn1_w_gate': nrm(ks[3], (DEPTH, D_MODEL, D_FF), D_MODEL ** -0.5),
        'ffn1_w_up': nrm(ks[4], (DEPTH, D_MODEL, D_FF), D_MODEL ** -0.5),
        'ffn1_w_down': nrm(ks[5], (DEPTH, D_FF, D_MODEL), D_FF ** -0.5),
        'mix_norm': gain(ks[6], (DEPTH, D_MODEL)),
        'w_in': nrm(ks[7], (DEPTH, D_MODEL, IN_WIDTH), D_MODEL ** -0.5),
        'a_q_norm': gain(ks[8], (DEPTH, HEAD_DIM)),
        'a_k_norm': gain(ks[9], (DEPTH, HEAD_DIM)),
        'b_lambda': nrm(ks[10], (DEPTH, 4, B_QK_DIM), 0.1),
        'b_subln': gain(ks[11], (DEPTH, B_V_DIM)),
        'c_rpb': nrm(ks[12], (DEPTH, C_HEADS, 2 * C_WIN_ROWS - 1, 2 * C_WIN_COLS - 1), 0.02),
        'w_out': nrm(ks[13], (DEPTH, MIX_WIDTH, D_MODEL), MIX_WIDTH ** -0.5),
        'ffn2_norm': gain(ks[14], (DEPTH, D_MODEL)),
        'ffn2_w_gate': nrm(ks[15], (DEPTH, D_MODEL, D_FF), D_MODEL ** -0.5),
        'ffn2_w_up': nrm(ks[16], (DEPTH, D_MODEL, D_FF), D_MODEL ** -0.5),
        'ffn2_w_down': nrm(ks[17], (DEPTH, D_FF, D_MODEL), D_FF ** -0.5),
        't5_table': nrm(ks[18], (T5_BUCKETS, T5_HEADS), 0.1),
        'final_norm': gain(ks[19], (D_MODEL,)),
    }


def reference(x_prompt, x_sample, ffn1_norm, ffn1_w_gate, ffn1_w_up, ffn1_w_down, mix_norm, w_in,
              a_q_norm, a_k_norm, b_lambda, b_subln, c_rpb, w_out, ffn2_norm, ffn2_w_gate, ffn2_w_up,
              ffn2_w_down, t5_table, final_norm):
    y_prompt = encoder_trunk(x_prompt, ffn1_norm, ffn1_w_gate, ffn1_w_up, ffn1_w_down, mix_norm, w_in,
                             a_q_norm, a_k_norm, b_lambda, b_subln, c_rpb, w_out, ffn2_norm, ffn2_w_gate,
                             ffn2_w_up, ffn2_w_down, t5_table, final_norm)
    y_sample = encoder_trunk(x_sample, ffn1_norm, ffn1_w_gate, ffn1_w_up, ffn1_w_down, mix_norm, w_in,
                             a_q_norm, a_k_norm, b_lambda, b_subln, c_rpb, w_out, ffn2_norm, ffn2_w_gate,
                             ffn2_w_up, ffn2_w_down, t5_table, final_norm)
    return (y_prompt, y_sample)
```

```python
import math
from contextlib import ExitStack, contextmanager
import numpy as np
import concourse.bass as bass
import concourse.mybir as mybir
from concourse.bass_utils import run_bass_kernel_spmd

F32 = mybir.dt.float32
BF16 = mybir.dt.bfloat16
AF = mybir.ActivationFunctionType
ALU = mybir.AluOpType
AX = mybir.AxisListType

ENGS = ("sync", "act", "dve", "pe", "pool")
EPOCH = 50000
NEG = -30000.0

T = 16384
NB = T // 128
NG = T // 512
D = 2048
FF = 5632
NF = FF // 128
KC = D // 128
DEPTH = 2
EPS = 1e-6
INW = 5376
WIN_SEGS = [(0, 0, 512, 1.0), (512, 768, 512, 0.125), (1024, 2304, 512, 0.125), (1536, 3840, 512, 0.125),
            (2048, 1280, 512, 1.0), (2560, 2816, 512, 1.0), (3072, 4352, 512, 1.0),
            (3584, 1792, 512, 1.0), (4096, 3328, 512, 1.0), (4608, 4864, 512, 1.0),
            (5120, 512, 128, 1.0), (5248, 640, 128, 1.0)]
CB, LB = 1151, 2304
CD, LD = 1535, 3072
WB_W = 2176
WD_W = 2944


class Buf:
    __slots__ = ("name", "t", "w", "r", "st")

    def __init__(self, name, t=None, st=None):
        self.name = name
        self.t = t
        self.w = None
        self.r = {}
        self.st = st

    def __getitem__(self, k):
        return self.t[k]


class Op:
    __slots__ = ("eng", "fn", "deps", "inc", "dma", "sem", "val")

    def __init__(self, eng, fn, dma):
        self.eng = eng
        self.fn = fn
        self.deps = []
        self.inc = False
        self.dma = dma
        self.sem = None
        self.val = 0


class Prog:
    def __init__(self, nc):
        self.nc = nc
        self.es = ExitStack()
        self.ops = {e: [] for e in ENGS}
        self.fence = None
        self.pending_stores = []
        self.semstate = {}
        self.scopes = []

    def sb(self, name, shape, dtype):
        st = self.scopes[-1] if self.scopes else self.es
        self.uid = getattr(self, "uid", 0) + 1
        t = st.enter_context(self.nc.sbuf_tensor("sb%d_%s" % (self.uid, name), list(shape), dtype))
        return Buf(name, t, self.semstate.setdefault(name, [None, 0, None, 0]))

    def ps(self, name, shape, dtype=F32):
        st = self.scopes[-1] if self.scopes else self.es
        self.uid = getattr(self, "uid", 0) + 1
        t = st.enter_context(self.nc.psum_tensor("ps%d_%s" % (self.uid, name), list(shape), dtype))
        return Buf(name, t, None)

    def newsem(self, name):
        self.sid = getattr(self, "sid", 0) + 1
        return self.es.enter_context(self.nc.semaphore("sem%d_%s" % (self.sid, name)))

    def _add(self, eng, fn, reads, writes, dma=False):
        op = Op(eng, fn, dma)
        deps = []
        for b in reads:
            if b.w is not None:
                deps.append(b.w)
        for b in writes:
            if b.w is not None:
                deps.append(b.w)
            deps.extend(b.r.values())
        seen = set()
        for d in deps:
            if id(d) in seen:
                continue
            seen.add(id(d))
            if d.eng == eng and not d.dma and eng == "pe":
                continue
            if not d.dma:
                d.inc = True
            op.deps.append(d)
        for b in reads:
            b.r[(eng, dma)] = op
        for b in writes:
            b.w = op
            b.r = {}
        self.ops[eng].append(op)
        return op

    def op(self, eng, fn, reads=(), writes=()):
        return self._add(eng, fn, reads, writes)

    def load(self, dst, fn, q="sync", nowaw=False):
        st = dst.st
        if st[0] is None:
            st[0] = self.newsem("d_" + dst.name)
        pw = dst.w
        if nowaw and pw is not None and pw.dma and pw.eng == q:
            dst.w = None
        op = self._add(q, fn, [], [dst], dma=True)
        if self.fence is not None:
            op.deps.append(self.fence)
        st[1] += 16
        assert st[1] < 98000, dst.name
        op.sem = st[0]
        op.val = st[1]
        return op

    def store(self, src, fn, q="pool"):
        st = src.st
        if st[2] is None:
            st[2] = self.newsem("s_" + src.name)
        op = self._add(q, fn, [src], [], dma=True)
        st[3] += 16
        assert st[3] < 98000, src.name
        op.sem = st[2]
        op.val = st[3]
        self.pending_stores.append(op)
        return op

    def barrier(self, q="pool"):
        mkp = self.markers["pool2"]
        op = Op(q, lambda g: g.memset(mkp[:], 0.0), False)
        if mkp.w is not None:
            op.deps.append(mkp.w)
        mkp.w = op
        last = {}
        for s in self.pending_stores:
            last[id(s.sem)] = s
        op.deps = op.deps + list(last.values())
        self.pending_stores = []
        self.ops[q].append(op)
        op.inc = True
        self.fence = op
        return op

    def full_barrier(self):
        self.barrier()
        toks = []
        mk = self.markers
        for e in ENGS:
            t = Buf("bar_" + e)
            if e == "dve":
                self.op(e, lambda h: h.memset(mk["dve"][:], 0.0), writes=[t, mk["dve"]])
            elif e == "pool":
                self.op(e, lambda h: h.memset(mk["pool"][:], 0.0), writes=[t, mk["pool"]])
            elif e == "act":
                self.op(e, lambda h: h.activation(out=mk["act"][:], in_=mk["zt"][:], func=AF.Copy), reads=[mk["zt"]], writes=[t, mk["act"]])
            elif e == "pe":
                pt_ = self.pe_tok
                self.op(e, lambda h, pt_=pt_: h.matmul(pt_[0:1, 0:1], lhsT=mk["ones16"][0:1, 0:1], rhs=mk["ones16"][0:1, 0:1], start=True, stop=True),
                        reads=[mk["ones16"]], writes=[t, pt_])
            else:
                self.op(e, lambda h: h.nop(nofuse=True), writes=[t])
            toks.append(t)
        for e in ENGS:
            self.op(e, lambda h: h.nop(nofuse=True), reads=toks)

    @contextmanager
    def scope(self):
        st = ExitStack()
        self.scopes.append(st)
        try:
            yield
        finally:
            self.full_barrier()
            self.scopes.pop()
            st.close()

    def emit(self):
        nc = self.nc
        esems = {}
        for e in ENGS:
            c = 0
            ep = 0
            cur = self.newsem("es_%s_0" % e)
            for op in self.ops[e]:
                if not op.dma and op.inc:
                    if c >= EPOCH:
                        ep += 1
                        c = 0
                        cur = self.newsem("es_%s_%d" % (e, ep))
                    c += 1
                    op.sem = cur
                    op.val = c
        hmap = {"sync": "sync", "act": "scalar", "dve": "vector", "pe": "tensor", "pool": "gpsimd"}
        with nc.Block() as block:
            for e in ENGS:
                ops = self.ops[e]

                def body(h, ops=ops):
                    waited = {}
                    for op in ops:
                        for d in op.deps:
                            k = id(d.sem)
                            if waited.get(k, 0) >= d.val:
                                continue
                            waited[k] = d.val
                            h.wait_ge(d.sem, d.val)
                        ins = op.fn(h)
                        if op.dma:
                            ins.then_inc(op.sem, 16)
                        elif op.inc:
                            ins.then_inc(op.sem, 1)

                getattr(block, hmap[e])(body)

    def close(self):
        self.es.close()


def build_program():
    nc = bass.Bass("TRN2", target_bir_lowering=False)

    def din(name, shape, dt=F32):
        return nc.dram_tensor(name, list(shape), dt, kind="ExternalInput")

    def dscr(name, shape, dt=BF16):
        return nc.dram_tensor(name, list(shape), dt)

    x_d = din("x", [T, D])
    y_d = nc.dram_tensor("y", [T, D], F32, kind="ExternalOutput")
    wgate_d = [din("ffn1_w_gate", [DEPTH, D, FF]), din("ffn2_w_gate", [DEPTH, D, FF])]
    wup_d = [din("ffn1_w_up", [DEPTH, D, FF]), din("ffn2_w_up", [DEPTH, D, FF])]
    wdown_d = [din("ffn1_w_down", [DEPTH, FF, D]), din("ffn2_w_down", [DEPTH, FF, D])]
    win_d = din("w_in", [DEPTH, D, INW])
    wout_d = din("w_out", [DEPTH, D, D])
    gains_d = din("gains_t", [DEPTH, 3, 128, KC])
    fin_d = din("final_norm", [1, D])
    aq_d = din("a_q_norm", [DEPTH, 64])
    ak_d = din("a_k_norm", [DEPTH, 64])
    bl_d = din("b_lambda", [DEPTH, 256])
    bs_d = din("b_subln_t", [DEPTH, 128, 1])
    rrev_d = din("rrev", [DEPTH * 8 * 15, 127])
    t5_d = din("t5_table", [32, 12])
    ident_d = din("ident", [128, 128])
    jmat_d = din("jmat", [128, 128])
    cos_d = din("cos_t", [T, 32])
    sin_d = din("sin_t", [T, 32])
    ohb_d = din("ohb", [32, LB])
    ohd_d = din("ohd", [32, LD])
    lm_d = din("lm", [1, LD])
    kmrow_d = din("kmrow", [1, T])
    maskc_d = din("maskc", [NG, 128, 8 * 512])

    xres = dscr("xres", [T, D], F32)
    WGU = [[dscr("wgu_%d_%d" % (l, w), [NF, 128, 2 * KC * 128]) for w in range(2)] for l in range(DEPTH)]
    WDN = [[dscr("wdn_%d_%d" % (l, w), [FF, D]) for w in range(2)] for l in range(DEPTH)]
    WIN = [dscr("win_%d" % l, [D, INW]) for l in range(DEPTH)]
    WOUT = [dscr("wout_%d" % l, [D, D]) for l in range(DEPTH)]
    QTA = dscr("qta", [2, 64, NB * 512])
    KTA = dscr("kta", [2, 64, T])
    QTX = [dscr("qt_%s" % m, [8, 64, T]) for m in "bcd"]
    KTX = [dscr("kt_%s" % m, [8, 64, T]) for m in "bcd"]
    VA = dscr("va", [T, 128])
    VX = [dscr("v_%s" % m, [T, 512]) for m in "bcd"]
    ATT = dscr("att_t", [D, T])
    OB = dscr("ob", [8, 128, T])
    GB = dscr("gb", [4, LB], F32)
    GD = dscr("gd", [8, LD], F32)

    import os as _os2
    LIM = int(_os2.environ.get("KLIM", "0"))
    NGr = LIM if LIM else NG

    def lim(seq):
        seq = list(seq)
        return seq[:LIM] if LIM else seq
    P = Prog(nc)
    def std_psum():
        pb_ = [P.ps("pb%d" % i, [128, 512]) for i in range(7)]
        P.pe_tok = pb_[6]
        return pb_, P.ps("pst", [128, 1024], BF16)

    ident = P.sb("ident", [128, 128], BF16)
    jm = P.sb("jm", [128, 128], BF16)
    ones32 = P.sb("ones32", [128, 128], F32)
    ones16 = P.sb("ones16", [128, 128], BF16)
    epst = P.sb("epst", [128, 1], F32)
    tl = P.sb("tl", [128, 12], F32)
    tr = P.sb("tr", [128, 12], F32)
    zt = P.sb("zt", [128, 1], F32)
    gains = P.sb("gains", [128, DEPTH * 3 * KC], F32)
    gfin = P.sb("gfin", [128, D], F32)
    small = P.sb("small", [128, 64], F32)

    P.markers = {"dve": P.sb("mk_dve", [128, 1], F32), "pool": P.sb("mk_pool", [128, 1], F32), "act": P.sb("mk_act", [128, 1], F32), "pool2": P.sb("mk_pool2", [128, 1], F32),
                 "zt": zt, "ones16": ones16}

    def gain_ap(l, which, kc):
        i = (l * 3 + which) * KC + kc
        return gains[:, i:i + 1]

    with P.scope():
        pb, pst = std_psum()
        c32 = P.sb("c32", [128, 256], F32)
        P.load(c32, lambda q: q.dma_start(out=c32[:, 0:128], in_=ident_d.ap()))
        P.op("dve", lambda e: e.tensor_copy(out=ident[:], in_=c32[:, 0:128]), reads=[c32], writes=[ident])
        c33 = P.sb("c33", [128, 128], F32)
        P.load(c33, lambda q: q.dma_start(out=c33[:], in_=jmat_d.ap()))
        P.op("dve", lambda e: e.tensor_copy(out=jm[:], in_=c33[:]), reads=[c33], writes=[jm])
        P.op("dve", lambda e: e.memset(ones32[:], 1.0), writes=[ones32])
        P.op("dve", lambda e: e.memset(ones16[:], 1.0), writes=[ones16])
        P.op("dve", lambda e: e.memset(epst[:], EPS), writes=[epst])
        P.op("dve", lambda e: e.memset(zt[:], 0.0), writes=[zt])
        P.load(gains, lambda q: q.dma_start(out=gains[:].rearrange("p (a k) -> p a k", k=KC),
                                            in_=gains_d.ap().rearrange("l w p k -> p (l w) k")))
        P.load(gfin, lambda q: q.dma_start(out=gfin[:], in_=fin_d.ap().partition_broadcast(128)))
        P.load(tl, lambda q: q.dma_start(out=tl[:], in_=t5_d.ap()[15:16, :].partition_broadcast(128)))
        P.load(tr, lambda q: q.dma_start(out=tr[:], in_=t5_d.ap()[31:32, :].partition_broadcast(128)))
        import os as _os
        KS = int(_os.environ.get("KSETUP", "9"))
        if KS >= 1:
            t5s = P.sb("t5s", [32, 12], F32)
            P.load(t5s, lambda q: q.dma_start(out=t5s[:], in_=t5_d.ap()))
            oh = P.sb("oh", [32, LD], F32)
            lmt = P.sb("lmt", [1, LD], F32)
            gsb = P.sb("gsb", [8, LD], F32)
            t5b = P.sb("t5b", [32, 12], BF16)
            ohb16 = P.sb("ohb16", [32, LD], BF16)
            lm16 = P.sb("lm16", [1, LD], BF16)
            P.op("dve", lambda e: e.tensor_copy(out=t5b[:], in_=t5s[:]), reads=[t5s], writes=[t5b])
            P.load(oh, lambda q: q.dma_start(out=oh[:, 0:LB], in_=ohb_d.ap()))
            P.op("dve", lambda e: e.tensor_copy(out=ohb16[:, 0:LB], in_=oh[:, 0:LB]), reads=[oh], writes=[ohb16])
            for c0 in range(0, LB, 512):
                cw = min(512, LB - c0)
                P.op("pe", lambda e, c0=c0, cw=cw: e.matmul(pb[0][0:4, 0:cw], lhsT=t5b[0:32, 0:4], rhs=ohb16[0:32, c0:c0 + cw],
                                                             start=True, stop=True), reads=[t5b, ohb16], writes=[pb[0]])
                P.op("dve", lambda e, c0=c0, cw=cw: e.tensor_copy(out=gsb[0:4, c0:c0 + cw], in_=pb[0][0:4, 0:cw]),
                     reads=[pb[0]], writes=[gsb])
            P.store(gsb, lambda q: q.dma_start(out=GB.ap(), in_=gsb[0:4, 0:LB]))
            P.load(oh, lambda q: q.dma_start(out=oh[:, 0:LD], in_=ohd_d.ap()))
            P.load(lmt, lambda q: q.dma_start(out=lmt[:], in_=lm_d.ap()))
            P.op("dve", lambda e: e.tensor_copy(out=ohb16[:, 0:LD], in_=oh[:, 0:LD]), reads=[oh], writes=[ohb16])
            P.op("dve", lambda e: e.tensor_copy(out=lm16[:], in_=lmt[:]), reads=[lmt], writes=[lm16])
            for c0 in range(0, LD, 512):
                def mm(e, c0=c0):
                    e.matmul(pb[0][0:8, :], lhsT=t5b[0:32, 4:12], rhs=ohb16[0:32, c0:c0 + 512], start=True, stop=False)
                    return e.matmul(pb[0][0:8, :], lhsT=ones16[0:1, 0:8], rhs=lm16[0:1, c0:c0 + 512], start=False, stop=True)
                P.op("pe", mm, reads=[t5b, ohb16, lm16, ones16], writes=[pb[0]])
                P.op("dve", lambda e, c0=c0: e.tensor_copy(out=gsb[0:8, c0:c0 + 512], in_=pb[0][0:8, :]),
                     reads=[pb[0]], writes=[gsb])
            P.store(gsb, lambda q: q.dma_start(out=GD.ap(), in_=gsb[0:8, :]))

        if KS >= 2:
            ld = [P.sb("wld%d" % i, [128, FF], F32) for i in range(2)]
            cv = [P.sb("wcv%d" % i, [128, FF], BF16) for i in range(2)]
            cnt = [0]

            def cast(dst_ap_fn, src_ap_fn, sc_ap, const, reads, writes):
                i = cnt[0]
                cnt[0] += 1
                if sc_ap is not None:
                    P.op("dve", lambda e: e.tensor_scalar_mul(out=dst_ap_fn(), in0=src_ap_fn(), scalar1=sc_ap), reads=reads, writes=writes)
                elif i % 2 == 0:
                    P.op("dve", lambda e: e.tensor_copy(out=dst_ap_fn(), in_=src_ap_fn()), reads=reads, writes=writes)
                else:
                    P.op("act", lambda e: e.activation(out=dst_ap_fn(), in_=src_ap_fn(), func=AF.Copy), reads=reads, writes=writes)

            it = [0]

            def slot():
                s = it[0] % 2
                it[0] += 1
                return ld[s], cv[s]

            for l in range(DEPTH):
                for w in range(2):
                    gw = 0 if w == 0 else 2
                    for m, src in enumerate((wgate_d[w], wup_d[w])):
                        for kc in range(KC):
                            a, b = slot()
                            P.load(a, lambda q, a=a, src=src, l=l, kc=kc: q.dma_start(out=a[:], in_=src.ap()[l, kc * 128:(kc + 1) * 128, :]))
                            cast(lambda b=b: b[:], lambda a=a: a[:], gain_ap(l, gw, kc), 1.0, [a, gains], [b])
                            P.store(b, lambda q, b=b, l=l, w=w, m=m, kc=kc: q.dma_start(
                                out=WGU[l][w].ap().rearrange("f p (m k j) -> f p m k j", m=2, k=KC)[:, :, m, kc, :].rearrange("f p j -> p f j"),
                                in_=b[:].rearrange("p (f j) -> p f j", j=128)))
                    for fc in range(NF):
                        a, b = slot()
                        P.load(a, lambda q, a=a, l=l, w=w, fc=fc: q.dma_start(out=a[:, 0:D], in_=wdown_d[w].ap()[l, fc * 128:(fc + 1) * 128, :]))
                        cast(lambda b=b: b[:, 0:D], lambda a=a: a[:, 0:D], None, 1.0, [a], [b])
                        P.store(b, lambda q, b=b, l=l, w=w, fc=fc: q.dma_start(out=WDN[l][w].ap()[fc * 128:(fc + 1) * 128, :], in_=b[:, 0:D]))
                for kc in range(KC):
                    a, b = slot()
                    P.load(a, lambda q, a=a, l=l, kc=kc: q.dma_start(out=a[:, 0:INW], in_=win_d.ap()[l, kc * 128:(kc + 1) * 128, :]))
                    for (dst, src, wd_, sc) in WIN_SEGS:
                        P.op("dve", lambda e, a=a, b=b, dst=dst, src=src, wd_=wd_, sc=sc, l=l, kc=kc: e.tensor_scalar(
                            out=b[:, dst:dst + wd_], in0=a[:, src:src + wd_], scalar1=gain_ap(l, 1, kc), scalar2=sc,
                            op0=ALU.mult, op1=ALU.mult), reads=[a, gains], writes=[b])
                    P.store(b, lambda q, b=b, l=l, kc=kc: q.dma_start(out=WIN[l].ap()[kc * 128:(kc + 1) * 128, :], in_=b[:, 0:INW]))
                for kc in range(KC):
                    a, b = slot()
                    P.load(a, lambda q, a=a, l=l, kc=kc: q.dma_start(out=a[:, 0:D], in_=wout_d.ap()[l, kc * 128:(kc + 1) * 128, :]))
                    cast(lambda b=b: b[:, 0:D], lambda a=a: a[:, 0:D], None, 1.0, [a], [b])
                    P.store(b, lambda q, b=b, l=l, kc=kc: q.dma_start(out=WOUT[l].ap()[kc * 128:(kc + 1) * 128, :], in_=b[:, 0:D]))

    def rstd_of(xb, xap_fn, junk, st, col):
        P.op("dve", lambda e: e.memset(st[:, col:col + 1], 0.0), writes=[st])
        P.op("act", lambda e: e.activation(out=junk[:], in_=xap_fn(), func=AF.Square, accum_out=st[:, col:col + 1]),
             reads=[xb, st], writes=[junk, st])
        P.op("act", lambda e: e.activation(out=st[:, 8 + col:9 + col], in_=st[:, col:col + 1], func=AF.Sqrt,
                                           scale=1.0 / D, bias=epst[:, 0:1]), reads=[st, epst], writes=[st])
        P.op("dve", lambda e: e.reciprocal(out=st[:, 16 + col:17 + col], in_=st[:, 8 + col:9 + col]), reads=[st], writes=[st])
        return lambda: st[:, 16 + col:17 + col]

    PS = {}

    def prep_group(g, src_dram, xs, hb, hT, junk, st):
        pst = PS["pst"]
        for tb in range(4):
            r0 = (g * 4 + tb) * 128
            P.load(xs[tb], lambda q, tb=tb, r0=r0: q.dma_start(out=xs[tb][:], in_=src_dram.ap()[r0:r0 + 128, :]))
        for tb in range(4):
            rs = rstd_of(xs[tb], lambda tb=tb: xs[tb][:], junk, st, tb)
            P.op("dve", lambda e, tb=tb, rs=rs: e.tensor_scalar_mul(out=hb[:], in0=xs[tb][:], scalar1=rs()),
                 reads=[xs[tb], st], writes=[hb])
            for half in range(2):
                def tr(e, half=half):
                    ins = None
                    for j in range(8):
                        kc = half * 8 + j
                        ins = e.transpose(pst[:, j * 128:(j + 1) * 128], hb[:, kc * 128:(kc + 1) * 128], ident[:])
                    return ins
                P.op("pe", tr, reads=[hb, ident], writes=[pst])
                eng = "dve"
                if eng == "act":
                    fn = lambda e, tb=tb, half=half: e.activation(
                        out=hT[:, half * 8:half * 8 + 8, tb * 128:(tb + 1) * 128],
                        in_=pst[:].rearrange("p (j t) -> p j t", j=8), func=AF.Copy)
                else:
                    fn = lambda e, tb=tb, half=half: e.tensor_copy(
                        out=hT[:, half * 8:half * 8 + 8, tb * 128:(tb + 1) * 128],
                        in_=pst[:].rearrange("p (j t) -> p j t", j=8))
                P.op(eng, fn, reads=[pst], writes=[hT])

    def ffn_phase(l, w, src_dram, final):
        with P.scope():
            pb, pst = std_psum()
            PS["pst"] = pst
            xs = [[P.sb("fx%d_%d" % (s, i), [128, D], F32) for i in range(4)] for s in range(2)]
            hb = P.sb("fhb", [128, D], BF16)
            hT = P.sb("fhT", [128, KC, 512], BF16)
            aT = P.sb("faT", [128, NF, 512], BF16)
            wgu = [P.sb("fwgu%d" % i, [128, 2, KC, 128], BF16) for i in range(3)]
            wdn = [P.sb("fwdn%d" % i, [128, 4, 512], BF16) for i in range(3)]
            sg = [P.sb("fsg%d" % i, [128, 512], F32) for i in range(2)]
            junk = P.sb("fjunk", [128, D], BF16)
            st = P.sb("fst", [128, 24], F32)
            gate_banks = [pb[4], pb[5], pb[6]]
            bi = [0]
            wi = [0]
            di = [0]
            gn = (0 if w == 0 else 2)

            def load_wgu(f):
                s = wgu[wi[0] % 3]
                wi[0] += 1
                P.load(s, lambda q, s=s, f=f: q.dma_start(out=s[:].rearrange("p m k j -> p (m k j)"), in_=WGU[l][w].ap()[f]))
                return s

            prep_group(0, src_dram, xs[0], hb, hT, junk, st)
            pend = load_wgu(0)
            for g in range(NGr):
                X = xs[g % 2]
                for f in range(NF):
                    ws = pend
                    if f + 1 < NF:
                        pend = load_wgu(f + 1)
                    elif g + 1 < NGr:
                        pend = load_wgu(0)
                    bg = gate_banks[bi[0] % 3]
                    bu = gate_banks[(bi[0] + 1) % 3]
                    bi[0] += 2
                    for m, bank in ((0, bg), (1, bu)):
                        def mm(e, ws=ws, m=m, bank=bank):
                            ins = None
                            for kc in range(KC):
                                ins = e.matmul(bank[:, :], lhsT=ws[:, m, kc, :], rhs=hT[:, kc, :], start=(kc == 0), stop=(kc == KC - 1))
                            return ins
                        P.op("pe", mm, reads=[ws, hT], writes=[bank])
                    sgt = sg[f % 2]
                    P.op("act", lambda e, sgt=sgt, bg=bg: e.activation(out=sgt[:], in_=bg[:, :], func=AF.Silu), reads=[bg], writes=[sgt])
                    P.op("dve", lambda e, sgt=sgt, bu=bu, f=f: e.tensor_tensor(out=aT[:, f, :], in0=bu[:, :], in1=sgt[:], op=ALU.mult),
                         reads=[bu, sgt], writes=[aT])
                if g + 1 < NGr:
                    prep_group(g + 1, src_dram, xs[(g + 1) % 2], hb, hT, junk, st)
                for n in range(4):
                    for fq in range(NF // 4):
                        s = wdn[di[0] % 3]
                        di[0] += 1
                        P.load(s, lambda q, s=s, fq=fq, n=n: q.dma_start(
                            out=s[:], in_=WDN[l][w].ap()[fq * 512:(fq + 1) * 512, n * 512:(n + 1) * 512].rearrange("(j p) c -> p j c", p=128)))

                        def mm(e, s=s, fq=fq):
                            ins = None
                            for j in range(4):
                                f = fq * 4 + j
                                for tb in range(4):
                                    ins = e.matmul(pb[tb][:, :], lhsT=aT[:, f, tb * 128:(tb + 1) * 128], rhs=s[:, j, :],
                                                   start=(f == 0), stop=(f == NF - 1))
                            return ins
                        P.op("pe", mm, reads=[s, aT], writes=[pb[0], pb[1], pb[2], pb[3]])
                    for tb in range(4):
                        P.op("dve", lambda e, tb=tb, n=n, X=X: e.scalar_tensor_tensor(
                            out=X[tb][:, n * 512:(n + 1) * 512], in0=pb[tb][:, :], scalar=0.5, in1=X[tb][:, n * 512:(n + 1) * 512],
                            op0=ALU.mult, op1=ALU.add), reads=[pb[tb], X[tb]], writes=[X[tb]])
                for tb in range(4):
                    r0 = (g * 4 + tb) * 128
                    if final:
                        rs = rstd_of(X[tb], lambda tb=tb, X=X: X[tb][:], junk, st, 4 + tb % 2)
                        P.op("dve", lambda e, tb=tb, X=X, rs=rs: e.scalar_tensor_tensor(
                            out=X[tb][:], in0=X[tb][:], scalar=rs(), in1=gfin[:], op0=ALU.mult, op1=ALU.mult),
                            reads=[X[tb], st, gfin], writes=[X[tb]])
                        P.store(X[tb], lambda q, tb=tb, X=X, r0=r0: q.dma_start(out=y_d.ap()[r0:r0 + 128, :], in_=X[tb][:]))
                    else:
                        P.store(X[tb], lambda q, tb=tb, X=X, r0=r0: q.dma_start(out=xres.ap()[r0:r0 + 128, :], in_=X[tb][:]))

    def proj_phase(l):
        with P.scope():
            pb, pst = std_psum()
            PS["pst"] = pst
            xs = [P.sb("px%d" % i, [128, D], F32) for i in range(4)]
            hb = P.sb("phb", [128, D], BF16)
            hT = P.sb("phT", [128, KC, 512], BF16)
            junk = P.sb("pjunk", [128, D], BF16)
            st = P.sb("pst_", [128, 24], F32)
            wch = [P.sb("pw%d" % i, [128, KC, 512], BF16) for i in range(2)]
            qk32 = P.sb("pqk32", [128, 4, 640], F32)
            tmp32 = P.sb("ptmp32", [128, 640], F32)
            tmp2 = P.sb("ptmp2", [128, 640], F32)
            qka = P.sb("pqka", [128, 4, 640], BF16)
            qkb = P.sb("pqkb", [128, 4, 6, 512], BF16)
            vb16 = P.sb("pvb", [128, 4, 3, 512], BF16)
            va16 = P.sb("pva", [128, 4, 128], BF16)
            cs = P.sb("pcs", [128, 4, 64], F32)
            gq8 = P.sb("pgq8", [128, 64], F32)
            gk = P.sb("pgk", [128, 64], F32)
            n10 = P.sb("pn10", [128, 32], F32)
            stg = [P.sb("pstg%d" % i, [128, 512], BF16) for i in range(4)]
            P.load(gq8, lambda q: q.dma_start(out=gq8[:], in_=aq_d.ap()[l:l + 1, :].partition_broadcast(128)))
            P.load(gk, lambda q: q.dma_start(out=gk[:], in_=ak_d.ap()[l:l + 1, :].partition_broadcast(128)))
            P.op("dve", lambda e: e.tensor_scalar_mul(out=gq8[:], in0=gq8[:], scalar1=0.125), reads=[gq8], writes=[gq8])
            wi = [0]
            si = [0]
            ei = [0]
            for g in range(NGr):
                prep_group(g, xres, xs, hb, hT, junk, st)
                P.load(cs, lambda q, g=g: q.dma_start(out=cs[:, :, 0:32], in_=cos_d.ap()[g * 512:(g + 1) * 512, :].rearrange("(t p) c -> p t c", p=128)))
                P.load(cs, lambda q, g=g: q.dma_start(out=cs[:, :, 32:64], in_=sin_d.ap()[g * 512:(g + 1) * 512, :].rearrange("(t p) c -> p t c", p=128)))
                for c in range(11):
                    cw = 512 if c < 10 else 256
                    ws = wch[wi[0] % 2]
                    wi[0] += 1
                    P.load(ws, lambda q, ws=ws, c=c, cw=cw: q.dma_start(
                        out=ws[:, :, 0:cw], in_=WIN[l].ap()[:, c * 512:c * 512 + cw].rearrange("(k p) c -> p k c", p=128)))
                    for tb in range(4):
                        bank = pb[(c * 4 + tb) % 4]

                        def mm(e, ws=ws, tb=tb, bank=bank, cw=cw):
                            ins = None
                            for kc in range(KC):
                                ins = e.matmul(bank[:, 0:cw], lhsT=hT[:, kc, tb * 128:(tb + 1) * 128], rhs=ws[:, kc, 0:cw],
                                               start=(kc == 0), stop=(kc == KC - 1))
                            return ins
                        P.op("pe", mm, reads=[ws, hT], writes=[bank])
                        eng = "dve"

                        def cp(eng, out_fn, in_fn, reads, writes):
                            if eng == "act":
                                P.op("act", lambda e: e.activation(out=out_fn(), in_=in_fn(), func=AF.Copy), reads=reads, writes=writes)
                            else:
                                P.op("dve", lambda e: e.tensor_copy(out=out_fn(), in_=in_fn()), reads=reads, writes=writes)
                        if c == 0:
                            cp(eng, lambda tb=tb: qk32[:, tb, 0:512], lambda bank=bank: bank[:, :], [bank], [qk32])
                        elif c <= 6:
                            cp(eng, lambda tb=tb, c=c: qkb[:, tb, c - 1, :], lambda bank=bank: bank[:, :], [bank], [qkb])
                        elif c <= 9:
                            cp(eng, lambda tb=tb, c=c: vb16[:, tb, c - 7, :], lambda bank=bank: bank[:, :], [bank], [vb16])
                        else:
                            cp("dve", lambda tb=tb: qk32[:, tb, 512:640], lambda bank=bank: bank[:, 0:128], [bank], [qk32])
                            cp("dve", lambda tb=tb: va16[:, tb, :], lambda bank=bank: bank[:, 128:256], [bank], [va16])
                P.store(va16, lambda q, g=g: q.dma_start(out=VA.ap()[g * 512:(g + 1) * 512, :].rearrange("(t p) c -> p t c", p=128), in_=va16[:]))
                for m in range(3):
                    P.store(vb16, lambda q, g=g, m=m: q.dma_start(out=VX[m].ap()[g * 512:(g + 1) * 512, :].rearrange("(t p) c -> p t c", p=128), in_=vb16[:, :, m, :]))
                for tb in range(4):
                    x3f = lambda tb=tb: qk32[:, tb, :].rearrange("p (h d) -> p h d", d=64)
                    t3f = lambda: tmp32[:].rearrange("p (h d) -> p h d", d=64)
                    a4f = lambda: tmp32[:].rearrange("p (h i two) -> p h i two", h=10, two=2)
                    j4f = lambda: tmp2[:].rearrange("p (h i two) -> p h i two", h=10, two=2)
                    o4f = lambda tb=tb: qka[:, tb, :].rearrange("p (h i two) -> p h i two", h=10, two=2)
                    ccf = lambda tb=tb: cs[:, tb, 0:32].unsqueeze(1).to_broadcast([128, 10, 32])
                    ssf = lambda tb=tb: cs[:, tb, 32:64].unsqueeze(1).to_broadcast([128, 10, 32])
                    P.op("dve", lambda e, tb=tb: e.tensor_tensor(out=tmp32[:], in0=qk32[:, tb, :], in1=qk32[:, tb, :], op=ALU.mult), reads=[qk32], writes=[tmp32])
                    P.op("dve", lambda e, t3f=t3f: e.reduce_sum(out=n10[:, 0:10], in_=t3f(), axis=AX.X), reads=[tmp32], writes=[n10])
                    P.op("act", lambda e: e.activation(out=n10[:, 10:20], in_=n10[:, 0:10], func=AF.Sqrt, scale=1.0 / 64, bias=epst[:, 0:1]),
                         reads=[n10, epst], writes=[n10])
                    P.op("dve", lambda e: e.reciprocal(out=n10[:, 20:30], in_=n10[:, 10:20]), reads=[n10], writes=[n10])
                    P.op("dve", lambda e, x3f=x3f, t3f=t3f: e.tensor_tensor(out=t3f(), in0=x3f(), in1=n10[:, 20:30].unsqueeze(2).to_broadcast([128, 10, 64]), op=ALU.mult),
                         reads=[qk32, n10], writes=[tmp32])
                    P.op("dve", lambda e, t3f=t3f: e.tensor_tensor(out=t3f()[:, 0:8, :], in0=t3f()[:, 0:8, :], in1=gq8[:].unsqueeze(1).to_broadcast([128, 8, 64]), op=ALU.mult),
                         reads=[tmp32, gq8], writes=[tmp32])
                    P.op("dve", lambda e, t3f=t3f: e.tensor_tensor(out=t3f()[:, 8:10, :], in0=t3f()[:, 8:10, :], in1=gk[:].unsqueeze(1).to_broadcast([128, 2, 64]), op=ALU.mult),
                         reads=[tmp32, gk], writes=[tmp32])
                    P.op("dve", lambda e, a4f=a4f, j4f=j4f, ccf=ccf: e.tensor_tensor(out=j4f()[:, :, :, 0], in0=a4f()[:, :, :, 0], in1=ccf(), op=ALU.mult), reads=[tmp32, cs], writes=[tmp2])
                    P.op("dve", lambda e, a4f=a4f, j4f=j4f, ssf=ssf: e.tensor_tensor(out=j4f()[:, :, :, 1], in0=a4f()[:, :, :, 1], in1=ssf(), op=ALU.mult), reads=[tmp32, cs], writes=[tmp2])
                    P.op("dve", lambda e, o4f=o4f, j4f=j4f: e.tensor_tensor(out=o4f()[:, :, :, 0], in0=j4f()[:, :, :, 0], in1=j4f()[:, :, :, 1], op=ALU.subtract), reads=[tmp2], writes=[qka])
                    P.op("dve", lambda e, a4f=a4f, j4f=j4f, ssf=ssf: e.tensor_tensor(out=j4f()[:, :, :, 0], in0=a4f()[:, :, :, 0], in1=ssf(), op=ALU.mult), reads=[tmp32, cs], writes=[tmp2])
                    P.op("dve", lambda e, a4f=a4f, j4f=j4f, ccf=ccf: e.tensor_tensor(out=j4f()[:, :, :, 1], in0=a4f()[:, :, :, 1], in1=ccf(), op=ALU.mult), reads=[tmp32, cs], writes=[tmp2])
                    P.op("dve", lambda e, o4f=o4f, j4f=j4f: e.tensor_tensor(out=o4f()[:, :, :, 1], in0=j4f()[:, :, :, 0], in1=j4f()[:, :, :, 1], op=ALU.add), reads=[tmp2], writes=[qka])
                chunks = []
                for i in range(5):
                    chunks.append(("a", i))
                for c in range(6):
                    for j in range(4):
                        chunks.append((c, j))
                for (kind, j) in chunks:
                    def tr(e, kind=kind, j=j):
                        ins = None
                        for tb in range(4):
                            src = qka[:, tb, j * 128:(j + 1) * 128] if kind == "a" else qkb[:, tb, kind, j * 128:(j + 1) * 128]
                            ins = e.transpose(pst[:, tb * 128:(tb + 1) * 128], src, ident[:])
                        return ins
                    P.op("pe", tr, reads=[qka if kind == "a" else qkb, ident], writes=[pst])
                    sg_ = stg[si[0] % 4]
                    si[0] += 1
                    if False:
                        P.op("act", lambda e, sg_=sg_: e.activation(out=sg_[:], in_=pst[:, 0:512], func=AF.Copy), reads=[pst], writes=[sg_])
                    else:
                        P.op("dve", lambda e, sg_=sg_: e.tensor_copy(out=sg_[:], in_=pst[:, 0:512]), reads=[pst], writes=[sg_])
                    for half in range(2):
                        hd = 2 * j + half
                        rows = slice(half * 64, half * 64 + 64)
                        if kind == "a" and j < 4:
                            dst = QTA.ap()[hd // 4].rearrange("d (b h q) -> d b h q", h=4, q=128)[:, g * 4:g * 4 + 4, hd % 4, :]
                            P.store(sg_, lambda q, sg_=sg_, rows=rows, dst=dst: q.dma_start(out=dst, in_=sg_[rows, :].rearrange("d (b q) -> d b q", q=128)))
                            continue
                        if kind == "a":
                            dst = KTA.ap()[half][:, g * 512:(g + 1) * 512]
                        elif kind < 3:
                            dst = QTX[kind].ap()[hd][:, g * 512:(g + 1) * 512]
                        else:
                            dst = KTX[kind - 3].ap()[hd][:, g * 512:(g + 1) * 512]
                        P.store(sg_, lambda q, sg_=sg_, rows=rows, dst=dst: q.dma_start(out=dst, in_=sg_[rows, :]))

    def att_phase(l):
        with P.scope():
            SG = [P.ps("sg0", [128, 1536]), P.ps("sg1", [128, 1536])]
            obank = P.ps("obank", [128, 512])
            misc = P.ps("misc", [128, 512])
            P.pe_tok = misc
            kt = P.sb("akt", [128, T], BF16)
            vt = P.sb("avt", [128, NB, 130], BF16)
            qt = [P.sb("aqt%d" % i, [128, 512], BF16) for i in range(2)]
            pt = [P.sb("apt%d" % i, [128, 1536], BF16) for i in range(2)]
            osb = P.sb("aosb", [128, 512], F32)
            rz = P.sb("arz", [128, 512], F32)
            o16 = [P.sb("ao16_%d" % i, [128, 512], BF16) for i in range(2)]
            wt32 = P.sb("awt32", [128, WD_W], F32)
            wt = P.sb("awt", [128, WD_W], BF16)
            rpb32 = P.sb("arpb", [128, 8, 512], F32)
            mk = P.sb("amk", [128, 8, 512], F32)
            cb16 = P.sb("acb", [128, 8, 512], BF16)
            kmt = P.sb("akmt", [128, 4096], F32)
            qi = [0]
            oi = [0]
            gi = [0]
            for c in range(4):
                P.load(kmt, lambda q, c=c: q.dma_start(out=kmt[64:65, :], in_=kmrow_d.ap()[:, c * 4096:(c + 1) * 4096]))
                P.op("dve", lambda e, c=c: e.tensor_copy(out=kt[64:65, c * 4096:(c + 1) * 4096], in_=kmt[64:65, :]), reads=[kmt], writes=[kt])
            for i in range(2):
                P.op("dve", lambda e, i=i: e.memset(qt[i][64:65, :], 1.0), reads=[], writes=[qt[i]])

            def bias_ap(bk):
                if bk is None:
                    return zt[:, 0:1]
                side, h = bk
                return tl[:, h:h + 1] if side == "l" else tr[:, h:h + 1]

            def flash(units, dv, zsep, store_fn):
                groups = []
                for ui, (qsrc, pairs) in enumerate(units):
                    gl = []
                    cur = []
                    for pr in pairs:
                        if cur and (len(cur) == 3 or cur[0][2] != pr[2]):
                            gl.append(cur)
                            cur = []
                        cur.append(pr)
                    if cur:
                        gl.append(cur)
                    for k, gpr in enumerate(gl):
                        groups.append((ui, k == 0, k == len(gl) - 1, gpr))
                qbufs = {}

                def qk(idx):
                    ui, first, last, gpr = groups[idx]
                    if first:
                        qb_ = qt[qi[0] % 2]
                        qi[0] += 1
                        qsrc = units[ui][0]
                        P.load(qb_, lambda q, qb_=qb_, qsrc=qsrc: q.dma_start(out=qb_[0:64, :], in_=qsrc))
                        qbufs[ui] = qb_
                    qb_ = qbufs[ui]
                    sg_ = SG[(gi[0] + idx) % 2]

                    def mm(e, gpr=gpr, sg_=sg_, qb_=qb_):
                        ins = None
                        for j, (kb, brhs, bk) in enumerate(gpr):
                            ins = e.matmul(sg_[:, j * 512:(j + 1) * 512], lhsT=kt[0:65, kb * 128:(kb + 1) * 128], rhs=qb_[0:65, :],
                                           start=True, stop=(brhs is None))
                            if brhs is not None:
                                ins = e.matmul(sg_[:, j * 512:(j + 1) * 512], lhsT=jm[:], rhs=brhs(), start=False, stop=True)
                        return ins
                    rd = [kt, qb_]
                    if gpr[0][1] is not None:
                        rd += [jm, wt, cb16]
                    P.op("pe", mm, reads=rd, writes=[sg_])

                qk(0)
                for idx in range(len(groups)):
                    ui, first, last, gpr = groups[idx]
                    if idx + 1 < len(groups):
                        qk(idx + 1)
                    n = len(gpr)
                    sg_ = SG[(gi[0] + idx) % 2]
                    p_ = pt[(gi[0] + idx) % 2]
                    bk = gpr[0][2]
                    P.op("act", lambda e, sg_=sg_, p_=p_, bk=bk, n=n: e.activation(out=p_[:, 0:n * 512], in_=sg_[:, 0:n * 512], func=AF.Exp,
                                                                             bias=bias_ap(bk), scale=1.0),
                         reads=[sg_, tl, tr, zt], writes=[p_])

                    def pv(e, gpr=gpr, p_=p_, first=first, last=last, n=n):
                        ins = None
                        for j, (kb, brhs, bk_) in enumerate(gpr):
                            st_ = first and j == 0
                            sp_ = last and j == n - 1
                            if zsep:
                                e.matmul(obank[0:dv, :], lhsT=vt[:, kb, 0:dv], rhs=p_[:, j * 512:(j + 1) * 512], start=st_, stop=sp_)
                                ins = e.matmul(misc[0:1, :], lhsT=ones16[:, 0:1], rhs=p_[:, j * 512:(j + 1) * 512], start=st_, stop=sp_)
                            else:
                                ins = e.matmul(obank[0:dv + 1, :], lhsT=vt[:, kb, 0:dv + 1], rhs=p_[:, j * 512:(j + 1) * 512], start=st_, stop=sp_)
                        return ins
                    P.op("pe", pv, reads=[vt, p_, ones16], writes=[obank, misc] if zsep else [obank])
                    if last:
                        zr = 0 if zsep else dv
                        zsrc = misc if zsep else obank

                        P.op("dve", lambda e, zsrc=zsrc, zr=zr: e.tensor_scalar_max(out=rz[zr:zr + 1, :], in0=zsrc[zr:zr + 1, :], scalar1=1e-30),
                             reads=[zsrc], writes=[rz])
                        P.op("dve", lambda e, zr=zr: e.reciprocal(out=rz[zr:zr + 1, :], in_=rz[zr:zr + 1, :]), reads=[rz], writes=[rz])
                        P.op("act", lambda e: e.activation(out=osb[0:dv, :], in_=obank[0:dv, :], func=AF.Copy), reads=[obank], writes=[osb])
                        P.op("pe", lambda e, zr=zr: e.matmul(misc[0:dv, :], lhsT=ones32[zr:zr + 1, 0:dv], rhs=rz[zr:zr + 1, :], start=True, stop=True),
                             reads=[rz, ones32], writes=[misc])
                        o_ = o16[oi[0] % 2]
                        oi[0] += 1
                        P.op("dve", lambda e, o_=o_: e.tensor_tensor(out=o_[0:dv, :], in0=osb[0:dv, :], in1=misc[0:dv, :], op=ALU.mult),
                             reads=[osb, misc], writes=[o_])
                        store_fn(ui, o_)
                gi[0] += len(groups)

            def load_k(src_ap):
                for c in range(4):
                    P.load(kt, lambda q, c=c: q.dma_start(out=kt[0:64, c * 4096:(c + 1) * 4096], in_=src_ap[:, c * 4096:(c + 1) * 4096]), nowaw=(c > 0))

            def load_v(src_ap, dv):
                for c in range(4):
                    P.load(vt, lambda q, c=c: q.dma_start(out=vt[:, c * 32:(c + 1) * 32, 0:dv],
                                                          in_=src_ap[c * 4096:(c + 1) * 4096, :].rearrange("(b p) d -> p b d", p=128)), nowaw=(c > 0))

            def load_w(off, width, tensor):
                P.load(wt32, lambda q: q.dma_start(out=wt32[:, 0:width], in_=bass.AP(tensor, off, [[1, 128], [1, width]])))
                P.op("dve", lambda e: e.tensor_copy(out=wt[:, 0:width], in_=wt32[:, 0:width]), reads=[wt32], writes=[wt])

            for gk_ in lim(range(2)):
                load_k(KTA.ap()[gk_])
                load_v(VA.ap()[:, gk_ * 64:(gk_ + 1) * 64], 64)
                P.op("dve", lambda e: e.memset(vt[:, :, 64:65], 1.0), reads=[], writes=[vt])
                units = []
                for qb in range(NB):
                    qsrc = QTA.ap()[gk_][:, qb * 512:(qb + 1) * 512]
                    units.append((qsrc, [(kb, None, None) for kb in range(NB)]))

                def st_a(ui, o_, gk_=gk_):
                    dst = ATT.ap()[gk_ * 256:(gk_ + 1) * 256, ui * 128:(ui + 1) * 128].rearrange("(h d) q -> d h q", d=64)
                    P.store(o_, lambda q, o_=o_, dst=dst: q.dma_start(out=dst, in_=o_[0:64, :].rearrange("d (h q) -> d h q", q=128)))
                flash(lim(units), 64, False, st_a)
            for hm in lim(range(8)):
                h = hm // 2
                load_k(KTX[0].ap()[hm])
                load_v(VX[0].ap()[:, h * 128:(h + 1) * 128], 128)
                load_w(h * LB, WB_W, GB)
                units = []
                for gq in range(NG):
                    qsrc = QTX[0].ap()[hm][:, gq * 512:(gq + 1) * 512]
                    pairs = []
                    for kb in range(NB):
                        dl = kb - 4 * gq
                        if -5 <= dl <= 8:
                            s_ = 1024 - 128 * dl
                            pairs.append((kb, (lambda s_=s_: wt[:, s_:s_ + 512]), None))
                        elif dl < 0:
                            pairs.append((kb, None, ("l", h)))
                        else:
                            pairs.append((kb, None, ("r", h)))
                    units.append((qsrc, pairs))

                def st_b(ui, o_, hm=hm):
                    dst = OB.ap()[hm][:, ui * 512:(ui + 1) * 512]
                    P.store(o_, lambda q, o_=o_, dst=dst: q.dma_start(out=dst, in_=o_[:, :]))
                flash(lim(units), 128, True, st_b)
            for h in lim(range(8)):
                load_k(KTX[1].ap()[h])
                load_v(VX[1].ap()[:, h * 64:(h + 1) * 64], 64)
                P.op("dve", lambda e: e.memset(vt[:, :, 64:65], 1.0), reads=[], writes=[vt])
                P.op("dve", lambda e: e.memset(rpb32[:], NEG), reads=[], writes=[rpb32])
                for dl in range(-2, 6):
                    for krl in range(2):
                        for qrl in range(8):
                            rr = 2 * dl + krl - qrl + 7
                            if rr < 0 or rr > 14:
                                continue
                            off = ((l * 8 + h) * 15 + rr) * 127
                            P.load(rpb32, lambda q, dl=dl, krl=krl, qrl=qrl, off=off: q.dma_start(
                                out=rpb32[(1 - krl) * 64:(1 - krl) * 64 + 64, dl + 2, qrl * 64:qrl * 64 + 64],
                                in_=bass.AP(rrev_d, off, [[1, 64], [1, 64]])), nowaw=True)
                for gq in lim(range(NG)):
                    qsrc = QTX[1].ap()[h][:, gq * 512:(gq + 1) * 512]
                    pairs = []
                    for dl in range(-2, 6):
                        kb = 4 * gq + dl
                        if kb < 0 or kb >= NB:
                            continue
                        pairs.append((kb, (lambda dl=dl: cb16[:, dl + 2, :]), None))
                    P.load(mk, lambda q, gq=gq: q.dma_start(out=mk[:].rearrange("p a b -> p (a b)"), in_=maskc_d.ap()[gq]))
                    P.op("dve", lambda e: e.tensor_tensor(out=cb16[:], in0=rpb32[:], in1=mk[:], op=ALU.add), reads=[rpb32, mk], writes=[cb16])

                    def st_c(ui, o_, gq=gq, h=h):
                        dst = ATT.ap()[1024 + h * 64:1024 + (h + 1) * 64, gq * 512:(gq + 1) * 512]
                        P.store(o_, lambda q, o_=o_, dst=dst: q.dma_start(out=dst, in_=o_[0:64, :]))
                    flash([(qsrc, pairs)], 64, False, st_c)
            for h in lim(range(8)):
                load_k(KTX[2].ap()[h])
                load_v(VX[2].ap()[:, h * 64:(h + 1) * 64], 64)
                P.op("dve", lambda e: e.memset(vt[:, :, 64:65], 1.0), reads=[], writes=[vt])
                load_w(h * LD, WD_W, GD)
                units = []
                for gq in range(NG):
                    qsrc = QTX[2].ap()[h][:, gq * 512:(gq + 1) * 512]
                    pairs = []
                    for dl in range(-8, 12):
                        kb = 4 * gq + dl
                        if kb < 0 or kb >= NB:
                            continue
                        s_ = 1408 - 128 * dl
                        pairs.append((kb, (lambda s_=s_: wt[:, s_:s_ + 512]), None))
                    units.append((qsrc, pairs))

                def st_d(ui, o_, h=h):
                    dst = ATT.ap()[1536 + h * 64:1536 + (h + 1) * 64, ui * 512:(ui + 1) * 512]
                    P.store(o_, lambda q, o_=o_, dst=dst: q.dma_start(out=dst, in_=o_[0:64, :]))
                flash(lim(units), 64, False, st_d)

    def wout_phase(l):
        lam_init = 0.8 - 0.6 * math.exp(-0.3 * l)
        with P.scope():
            pb, pst = std_psum()
            xs1 = [P.sb("ox%d" % i, [128, D], F32) for i in range(4)]
            xs = [xs1, xs1]
            mixT = [P.sb("omix%d" % i, [128, KC, 512], BF16) for i in range(2)]
            o12 = [P.sb("oo12_%d" % i, [128, 2, 4, 512], BF16) for i in range(2)]
            wo = P.sb("owo", [128, KC, D], BF16)
            d32 = P.sb("od32", [128, 512], F32)
            sq = P.sb("osq", [128, 512], F32)
            rt = P.sb("ort", [128, 512], F32)
            bl = P.sb("obl", [128, 256], F32)
            lam = P.sb("olam", [128, 8], F32)
            gsub = P.sb("ogsub", [128, 1], F32)
            P.load(wo, lambda q: q.dma_start(out=wo[:], in_=WOUT[l].ap().rearrange("(k p) c -> p k c", p=128)))
            P.load(bl, lambda q: q.dma_start(out=bl[:], in_=bl_d.ap()[l:l + 1, :].partition_broadcast(128)))
            P.load(gsub, lambda q: q.dma_start(out=gsub[:], in_=bs_d.ap()[l]))

            P.op("dve", lambda e: e.tensor_tensor(out=bl[:, 0:64], in0=bl[:, 0:64], in1=bl[:, 64:128], op=ALU.mult), reads=[bl], writes=[bl])
            P.op("dve", lambda e: e.tensor_tensor(out=bl[:, 128:192], in0=bl[:, 128:192], in1=bl[:, 192:256], op=ALU.mult), reads=[bl], writes=[bl])
            P.op("dve", lambda e: e.reduce_sum(out=lam[:, 0:1], in_=bl[:, 0:64], axis=AX.X), reads=[bl], writes=[lam])
            P.op("dve", lambda e: e.reduce_sum(out=lam[:, 1:2], in_=bl[:, 128:192], axis=AX.X), reads=[bl], writes=[lam])
            P.op("act", lambda e: e.activation(out=lam[:, 2:4], in_=lam[:, 0:2], func=AF.Exp), reads=[lam], writes=[lam])
            P.op("dve", lambda e: e.tensor_tensor(out=lam[:, 4:5], in0=lam[:, 3:4], in1=lam[:, 2:3], op=ALU.subtract), reads=[lam], writes=[lam])
            P.op("dve", lambda e: e.tensor_scalar_add(out=lam[:, 5:6], in0=lam[:, 4:5], scalar1=-lam_init), reads=[lam], writes=[lam])
            P.op("dve", lambda e: e.tensor_scalar_mul(out=gsub[:], in0=gsub[:], scalar1=(1.0 - lam_init)), reads=[gsub], writes=[gsub])

            def loads_x(g):
                X = xs[g % 2]
                for tb in range(4):
                    r0 = (g * 4 + tb) * 128
                    P.load(X[tb], lambda q, tb=tb, r0=r0, X=X: q.dma_start(out=X[tb][:], in_=xres.ap()[r0:r0 + 128, :]))

            def loads(g):
                M = mixT[g % 2]
                O = o12[g % 2]
                for (c0, r0) in ((0, 0), (8, 1024), (12, 1536)):
                    P.load(M, lambda q, c0=c0, r0=r0, M=M, g=g: q.dma_start(
                        out=M[:, c0:c0 + 4, :], in_=ATT.ap()[r0:r0 + 512, g * 512:(g + 1) * 512].rearrange("(c p) t -> p c t", p=128)))
                for m in range(2):
                    P.load(O, lambda q, m=m, O=O, g=g: q.dma_start(
                        out=O[:, m, :, :], in_=OB.ap().rearrange("(h m) p t -> m p h t", m=2)[m][:, :, g * 512:(g + 1) * 512]))

            loads(0)
            for g in range(NGr):
                X = xs[g % 2]
                M = mixT[g % 2]
                O = o12[g % 2]
                if g + 1 < NGr:
                    loads(g + 1)
                loads_x(g)
                for h in range(4):
                    P.op("dve", lambda e, h=h, O=O: e.scalar_tensor_tensor(out=d32[:], in0=O[:, 1, h, :], scalar=lam[:, 5:6], in1=O[:, 0, h, :],
                                                                         op0=ALU.mult, op1=ALU.add), reads=[O, lam], writes=[d32])
                    P.op("act", lambda e: e.activation(out=sq[:], in_=d32[:], func=AF.Square), reads=[d32], writes=[sq])
                    P.op("pe", lambda e: e.matmul(pb[4][:, :], lhsT=ones32[:, :], rhs=sq[:], start=True, stop=True), reads=[ones32, sq], writes=[pb[4]])
                    P.op("act", lambda e: e.activation(out=rt[:], in_=pb[4][:, :], func=AF.Sqrt, scale=1.0 / 128, bias=epst[:, 0:1]),
                         reads=[pb[4], epst], writes=[rt])
                    P.op("dve", lambda e: e.reciprocal(out=rt[:], in_=rt[:]), reads=[rt], writes=[rt])
                    P.op("dve", lambda e, h=h, M=M: e.scalar_tensor_tensor(out=M[:, 4 + h, :], in0=d32[:], scalar=gsub[:, 0:1], in1=rt[:],
                                                                         op0=ALU.mult, op1=ALU.mult), reads=[d32, gsub, rt], writes=[M])
                for tb in range(4):
                    for n in range(4):
                        bank = pb[(tb * 4 + n) % 4]

                        def mm(e, tb=tb, n=n, bank=bank, M=M):
                            ins = None
                            for c in range(KC):
                                ins = e.matmul(bank[:, :], lhsT=M[:, c, tb * 128:(tb + 1) * 128], rhs=wo[:, c, n * 512:(n + 1) * 512],
                                               start=(c == 0), stop=(c == KC - 1))
                            return ins
                        P.op("pe", mm, reads=[M, wo], writes=[bank])
                        P.op("dve", lambda e, tb=tb, n=n, bank=bank, X=X: e.tensor_tensor(
                            out=X[tb][:, n * 512:(n + 1) * 512], in0=bank[:, :], in1=X[tb][:, n * 512:(n + 1) * 512], op=ALU.add),
                            reads=[bank, X[tb]], writes=[X[tb]])
                    r0 = (g * 4 + tb) * 128
                    P.store(X[tb], lambda q, tb=tb, X=X, r0=r0: q.dma_start(out=xres.ap()[r0:r0 + 128, :], in_=X[tb][:]))

    import os
    PH = os.environ.get("KPH", "ffn1,proj,att,wout,ffn2").split(",")
    for l in range(int(os.environ.get("KDEPTH", DEPTH))):
        if "ffn1" in PH:
            ffn_phase(l, 0, x_d if l == 0 else xres, False)
        if "proj" in PH:
            proj_phase(l)
        if "att" in PH:
            att_phase(l)
        if "wout" in PH:
            wout_phase(l)
        if "ffn2" in PH:
            ffn_phase(l, 1, xres, l == DEPTH - 1)
    P.barrier()
    P.emit()
    P.close()
    return nc


def _t5_bucket_np(rel):
    nb = 16
    max_exact = 8
    side = (rel > 0).astype(np.int32) * nb
    n = np.abs(rel)
    nf = np.maximum(n, 1).astype(np.float32) / np.float32(max_exact)
    large = max_exact + (np.log(nf).astype(np.float32) / np.float32(math.log(1024 / max_exact)) * np.float32(nb - max_exact)).astype(np.int32)
    large = np.minimum(large, nb - 1)
    return side + np.where(n < max_exact, n, large)


def _consts(t_real):
    c = {}
    c["ident"] = np.eye(128, dtype=np.float32)
    c["jmat"] = np.ascontiguousarray(np.eye(128, dtype=np.float32)[::-1])
    t = np.arange(T, dtype=np.int32)
    inv_freq = (10000.0 ** (-np.arange(16, dtype=np.float32) / 16)).astype(np.float32)
    ang = np.concatenate([(t // 64).astype(np.float32)[:, None] * inv_freq[None, :],
                          (t % 64).astype(np.float32)[:, None] * inv_freq[None, :]], axis=-1).astype(np.float32)
    c["cos_t"] = np.cos(ang).astype(np.float32)
    c["sin_t"] = np.sin(ang).astype(np.float32)
    relb = CB - np.arange(LB)
    bb = _t5_bucket_np(relb)
    ohb = np.zeros((32, LB), np.float32)
    ohb[bb, np.arange(LB)] = 1.0
    c["ohb"] = ohb
    reld = CD - np.arange(LD)
    bd = _t5_bucket_np(reld)
    a = np.abs(reld)
    mult = (a <= 64).astype(np.int32) + ((a <= 256) & (reld % 4 == 0)).astype(np.int32) + ((a <= 1024) & (reld % 16 == 0)).astype(np.int32)
    ohd = np.zeros((32, LD), np.float32)
    ohd[bd, np.arange(LD)] = 1.0
    ohd[:, mult == 0] = 0.0
    c["ohd"] = ohd
    lm = np.where(mult > 0, np.log(np.maximum(mult, 1).astype(np.float32)), np.float32(NEG)).astype(np.float32)
    c["lm"] = lm[None, :]
    kmr_ = np.zeros((1, T), np.float32)
    kmr_[0, t_real:] = NEG
    c["kmrow"] = kmr_
    rows_real = t_real // 64
    gq = np.arange(NG)[:, None, None, None]
    pp = np.arange(128)[None, :, None, None]
    di = np.arange(8)[None, None, :, None]
    ql = np.arange(512)[None, None, None, :]
    kl = 127 - pp
    krl, kc = kl // 64, kl % 64
    qrl, qc = ql // 64, ql % 64
    kb = 4 * gq + (di - 2)
    kr = 2 * kb + krl
    r = 8 * gq + qrl
    rows_eff = np.where(r < rows_real, rows_real, T // 64)
    rs = np.clip(r - 4, 0, rows_eff - 8)
    cs_ = np.clip(qc - 8, 0, 48)
    valid = (kr >= rs) & (kr < rs + 8) & (kc >= cs_) & (kc < cs_ + 16) & (kb >= 0) & (kb < NB)
    c["maskc"] = np.where(valid, np.float32(0.0), np.float32(NEG)).astype(np.float32).reshape(NG, 128, 8 * 512)
    return c


_NC_CACHE = {}


def kernel(**inp):
    f = lambda a: np.ascontiguousarray(np.asarray(a, dtype=np.float32))
    xp = f(inp["x_prompt"])
    xsm = f(inp["x_sample"])
    seqs = [xp[0], xp[1], xsm[0], xsm[1]]
    treal = [xp.shape[1], xp.shape[1], xsm.shape[1], xsm.shape[1]]
    shared = {}
    for k in ("ffn1_w_gate", "ffn1_w_up", "ffn1_w_down", "ffn2_w_gate", "ffn2_w_up", "ffn2_w_down", "w_in", "w_out",
              "a_q_norm", "a_k_norm", "t5_table"):
        shared[k] = f(inp[k])
    g3 = np.stack([f(inp["ffn1_norm"]), f(inp["mix_norm"]), f(inp["ffn2_norm"])], axis=1)
    shared["gains_t"] = np.ascontiguousarray(g3.reshape(DEPTH, 3, KC, 128).transpose(0, 1, 3, 2))
    shared["final_norm"] = f(inp["final_norm"]).reshape(1, D)
    shared["b_lambda"] = f(inp["b_lambda"]).reshape(DEPTH, 256)
    shared["b_subln_t"] = f(inp["b_subln"]).reshape(DEPTH, 128, 1)
    rpb = f(inp["c_rpb"])
    rrev = np.full((DEPTH, 8, 15, 127), NEG, np.float32)
    rrev[..., 48:79] = rpb[..., ::-1]
    shared["rrev"] = rrev.reshape(DEPTH * 8 * 15, 127)
    cc = {t_: _consts(t_) for t_ in set(treal)}
    in_maps = []
    for c in range(8):
        s = c % 4
        x = np.zeros((T, D), np.float32)
        x[:treal[s]] = seqs[s]
        m = dict(shared)
        m.update(cc[treal[s]])
        m["x"] = x
        in_maps.append(m)
    if "nc" not in _NC_CACHE:
        _NC_CACHE["nc"] = build_program()
    res = run_bass_kernel_spmd(_NC_CACHE["nc"], in_maps, core_ids=list(range(8)))
    ys = [np.asarray(res.results[c]["y"]) for c in range(4)]
    y_prompt = np.stack([ys[0][:treal[0]], ys[1][:treal[1]]]).astype(np.float32)
    y_sample = np.stack([ys[2][:treal[2]], ys[3][:treal[3]]]).astype(np.float32)
    return (y_prompt, y_sample)
```

```python
import math
from contextlib import ExitStack, contextmanager
import numpy as np
import concourse.bass as bass
import concourse.mybir as mybir
from concourse.bass_utils import run_bass_kernel_spmd

F32 = mybir.dt.float32
BF16 = mybir.dt.bfloat16
AF = mybir.ActivationFunctionType
ALU = mybir.AluOpType
AX = mybir.AxisListType

ENGS = ("sync", "act", "dve", "pe", "pool")
EPOCH = 50000
NEG = -30000.0

T = 16384
NB = T // 128
NG = T // 512
D = 2048
FF = 5632
NF = FF // 128
KC = D // 128
DEPTH = 2
EPS = 1e-6
INW = 5376
WIN_SEGS = [(0, 0, 512, 1.0), (512, 768, 512, 0.125), (1024, 2304, 512, 0.125), (1536, 3840, 512, 0.125),
            (2048, 1280, 512, 1.0), (2560, 2816, 512, 1.0), (3072, 4352, 512, 1.0),
            (3584, 1792, 512, 1.0), (4096, 3328, 512, 1.0), (4608, 4864, 512, 1.0),
            (5120, 512, 128, 1.0), (5248, 640, 128, 1.0)]
CB, LB = 1151, 2304
CD, LD = 1535, 3072
WB_W = 2176
WD_W = 2944


class Buf:
    __slots__ = ("name", "t", "w", "r", "st")

    def __init__(self, name, t=None, st=None):
        self.name = name
        self.t = t
        self.w = None
        self.r = {}
        self.st = st

    def __getitem__(self, k):
        return self.t[k]


class Op:
    __slots__ = ("eng", "fn", "deps", "inc", "dma", "sem", "val")

    def __init__(self, eng, fn, dma):
        self.eng = eng
        self.fn = fn
        self.deps = []
        self.inc = False
        self.dma = dma
        self.sem = None
        self.val = 0


class Prog:
    def __init__(self, nc):
        self.nc = nc
        self.es = ExitStack()
        self.ops = {e: [] for e in ENGS}
        self.fence = None
        self.pending_stores = []
        self.semstate = {}
        self.scopes = []

    def sb(self, name, shape, dtype):
        st = self.scopes[-1] if self.scopes else self.es
        self.uid = getattr(self, "uid", 0) + 1
        t = st.enter_context(self.nc.sbuf_tensor("sb%d_%s" % (self.uid, name), list(shape), dtype))
        return Buf(name, t, self.semstate.setdefault(name, [None, 0, None, 0]))

    def ps(self, name, shape, dtype=F32):
        st = self.scopes[-1] if self.scopes else self.es
        self.uid = getattr(self, "uid", 0) + 1
        t = st.enter_context(self.nc.psum_tensor("ps%d_%s" % (self.uid, name), list(shape), dtype))
        return Buf(name, t, None)

    def newsem(self, name):
        self.sid = getattr(self, "sid", 0) + 1
        return self.es.enter_context(self.nc.semaphore("sem%d_%s" % (self.sid, name)))

    def _add(self, eng, fn, reads, writes, dma=False):
        op = Op(eng, fn, dma)
        deps = []
        for b in reads:
            if b.w is not None:
                deps.append(b.w)
        for b in writes:
            if b.w is not None:
                deps.append(b.w)
            deps.extend(b.r.values())
        seen = set()
        for d in deps:
            if id(d) in seen:
                continue
            seen.add(id(d))
            if d.eng == eng and not d.dma and eng == "pe":
                continue
            if not d.dma:
                d.inc = True
            op.deps.append(d)
        for b in reads:
            b.r[(eng, dma)] = op
        for b in writes:
            b.w = op
            b.r = {}
        self.ops[eng].append(op)
        return op

    def op(self, eng, fn, reads=(), writes=()):
        return self._add(eng, fn, reads, writes)

    def load(self, dst, fn, q="sync", nowaw=False):
        st = dst.st
        if st[0] is None:
            st[0] = self.newsem("d_" + dst.name)
        pw = dst.w
        if nowaw and pw is not None and pw.dma and pw.eng == q:
            dst.w = None
        op = self._add(q, fn, [], [dst], dma=True)
        if self.fence is not None:
            op.deps.append(self.fence)
        st[1] += 16
        assert st[1] < 98000, dst.name
        op.sem = st[0]
        op.val = st[1]
        return op

    def store(self, src, fn, q="pool"):
        st = src.st
        if st[2] is None:
            st[2] = self.newsem("s_" + src.name)
        op = self._add(q, fn, [src], [], dma=True)
        st[3] += 16
        assert st[3] < 98000, src.name
        op.sem = st[2]
        op.val = st[3]
        self.pending_stores.append(op)
        return op

    def barrier(self, q="pool"):
        mkp = self.markers["pool2"]
        op = Op(q, lambda g: g.memset(mkp[:], 0.0), False)
        if mkp.w is not None:
            op.deps.append(mkp.w)
        mkp.w = op
        last = {}
        for s in self.pending_stores:
            last[id(s.sem)] = s
        op.deps = op.deps + list(last.values())
        self.pending_stores = []
        self.ops[q].append(op)
        op.inc = True
        self.fence = op
        return op

    def full_barrier(self):
        self.barrier()
        toks = []
        mk = self.markers
        for e in ENGS:
            t = Buf("bar_" + e)
            if e == "dve":
                self.op(e, lambda h: h.memset(mk["dve"][:], 0.0), writes=[t, mk["dve"]])
            elif e == "pool":
                self.op(e, lambda h: h.memset(mk["pool"][:], 0.0), writes=[t, mk["pool"]])
            elif e == "act":
                self.op(e, lambda h: h.activation(out=mk["act"][:], in_=mk["zt"][:], func=AF.Copy), reads=[mk["zt"]], writes=[t, mk["act"]])
            elif e == "pe":
                pt_ = self.pe_tok
                self.op(e, lambda h, pt_=pt_: h.matmul(pt_[0:1, 0:1], lhsT=mk["ones16"][0:1, 0:1], rhs=mk["ones16"][0:1, 0:1], start=True, stop=True),
                        reads=[mk["ones16"]], writes=[t, pt_])
            else:
                self.op(e, lambda h: h.nop(nofuse=True), writes=[t])
            toks.append(t)
        for e in ENGS:
            self.op(e, lambda h: h.nop(nofuse=True), reads=toks)

    @contextmanager
    def scope(self):
        st = ExitStack()
        self.scopes.append(st)
        try:
            yield
        finally:
            self.full_barrier()
            self.scopes.pop()
            st.close()

    def emit(self):
        nc = self.nc
        esems = {}
        for e in ENGS:
            c = 0
            ep = 0
            cur = self.newsem("es_%s_0" % e)
            for op in self.ops[e]:
                if not op.dma and op.inc:
                    if c >= EPOCH:
                        ep += 1
                        c = 0
                        cur = self.newsem("es_%s_%d" % (e, ep))
                    c += 1
                    op.sem = cur
                    op.val = c
        hmap = {"sync": "sync", "act": "scalar", "dve": "vector", "pe": "tensor", "pool": "gpsimd"}
        with nc.Block() as block:
            for e in ENGS:
                ops = self.ops[e]

                def body(h, ops=ops):
                    waited = {}
                    for op in ops:
                        for d in op.deps:
                            k = id(d.sem)
                            if waited.get(k, 0) >= d.val:
                                continue
                            waited[k] = d.val
                            h.wait_ge(d.sem, d.val)
                        ins = op.fn(h)
                        if op.dma:
                            ins.then_inc(op.sem, 16)
                        elif op.inc:
                            ins.then_inc(op.sem, 1)

                getattr(block, hmap[e])(body)

    def close(self):
        self.es.close()


def build_program():
    nc = bass.Bass("TRN2", target_bir_lowering=False)

    def din(name, shape, dt=F32):
        return nc.dram_tensor(name, list(shape), dt, kind="ExternalInput")

    def dscr(name, shape, dt=BF16):
        return nc.dram_tensor(name, list(shape), dt)

    x_d = din("x", [T, D])
    y_d = nc.dram_tensor("y", [T, D], F32, kind="ExternalOutput")
    wgate_d = [din("ffn1_w_gate", [DEPTH, D, FF]), din("ffn2_w_gate", [DEPTH, D, FF])]
    wup_d = [din("ffn1_w_up", [DEPTH, D, FF]), din("ffn2_w_up", [DEPTH, D, FF])]
    wdown_d = [din("ffn1_w_down", [DEPTH, FF, D]), din("ffn2_w_down", [DEPTH, FF, D])]
    win_d = din("w_in", [DEPTH, D, INW])
    wout_d = din("w_out", [DEPTH, D, D])
    gains_d = din("gains_t", [DEPTH, 3, 128, KC])
    fin_d = din("final_norm", [1, D])
    aq_d = din("a_q_norm", [DEPTH, 64])
    ak_d = din("a_k_norm", [DEPTH, 64])
    bl_d = din("b_lambda", [DEPTH, 256])
    bs_d = din("b_subln_t", [DEPTH, 128, 1])
    rrev_d = din("rrev", [DEPTH * 8 * 15, 127])
    t5_d = din("t5_table", [32, 12])
    ident_d = din("ident", [128, 128])
    jmat_d = din("jmat", [128, 128])
    cos_d = din("cos_t", [T, 32])
    sin_d = din("sin_t", [T, 32])
    ohb_d = din("ohb", [32, LB])
    ohd_d = din("ohd", [32, LD])
    lm_d = din("lm", [1, LD])
    kmrow_d = din("kmrow", [1, T])
    maskc_d = din("maskc", [NG, 128, 8 * 512])

    xres = dscr("xres", [T, D], F32)
    WGU = [[dscr("wgu_%d_%d" % (l, w), [NF, 128, 2 * KC * 128]) for w in range(2)] for l in range(DEPTH)]
    WDN = [[dscr("wdn_%d_%d" % (l, w), [FF, D]) for w in range(2)] for l in range(DEPTH)]
    WIN = [dscr("win_%d" % l, [D, INW]) for l in range(DEPTH)]
    WOUT = [dscr("wout_%d" % l, [D, D]) for l in range(DEPTH)]
    QTA = dscr("qta", [2, 64, NB * 512])
    KTA = dscr("kta", [2, 64, T])
    QTX = [dscr("qt_%s" % m, [8, 64, T]) for m in "bcd"]
    KTX = [dscr("kt_%s" % m, [8, 64, T]) for m in "bcd"]
    VA = dscr("va", [T, 128])
    VX = [dscr("v_%s" % m, [T, 512]) for m in "bcd"]
    ATT = dscr("att_t", [D, T])
    OB = dscr("ob", [8, 128, T])
    GB = dscr("gb", [4, LB], F32)
    GD = dscr("gd", [8, LD], F32)

    import os as _os2
    LIM = int(_os2.environ.get("KLIM", "0"))
    NGr = LIM if LIM else NG

    def lim(seq):
        seq = list(seq)
        return seq[:LIM] if LIM else seq
    P = Prog(nc)
    def std_psum():
        pb_ = [P.ps("pb%d" % i, [128, 512]) for i in range(7)]
        P.pe_tok = pb_[6]
        return pb_, P.ps("pst", [128, 1024], BF16)

    ident = P.sb("ident", [128, 128], BF16)
    jm = P.sb("jm", [128, 128], BF16)
    ones32 = P.sb("ones32", [128, 128], F32)
    ones16 = P.sb("ones16", [128, 128], BF16)
    epst = P.sb("epst", [128, 1], F32)
    tl = P.sb("tl", [128, 12], F32)
    tr = P.sb("tr", [128, 12], F32)
    zt = P.sb("zt", [128, 1], F32)
    gains = P.sb("gains", [128, DEPTH * 3 * KC], F32)
    gfin = P.sb("gfin", [128, D], F32)
    small = P.sb("small", [128, 64], F32)

    P.markers = {"dve": P.sb("mk_dve", [128, 1], F32), "pool": P.sb("mk_pool", [128, 1], F32), "act": P.sb("mk_act", [128, 1], F32), "pool2": P.sb("mk_pool2", [128, 1], F32),
                 "zt": zt, "ones16": ones16}

    def gain_ap(l, which, kc):
        i = (l * 3 + which) * KC + kc
        return gains[:, i:i + 1]

    with P.scope():
        pb, pst = std_psum()
        c32 = P.sb("c32", [128, 256], F32)
        P.load(c32, lambda q: q.dma_start(out=c32[:, 0:128], in_=ident_d.ap()))
        P.op("dve", lambda e: e.tensor_copy(out=ident[:], in_=c32[:, 0:128]), reads=[c32], writes=[ident])
        c33 = P.sb("c33", [128, 128], F32)
        P.load(c33, lambda q: q.dma_start(out=c33[:], in_=jmat_d.ap()))
        P.op("dve", lambda e: e.tensor_copy(out=jm[:], in_=c33[:]), reads=[c33], writes=[jm])
        P.op("dve", lambda e: e.memset(ones32[:], 1.0), writes=[ones32])
        P.op("dve", lambda e: e.memset(ones16[:], 1.0), writes=[ones16])
        P.op("dve", lambda e: e.memset(epst[:], EPS), writes=[epst])
        P.op("dve", lambda e: e.memset(zt[:], 0.0), writes=[zt])
        P.load(gains, lambda q: q.dma_start(out=gains[:].rearrange("p (a k) -> p a k", k=KC),
                                            in_=gains_d.ap().rearrange("l w p k -> p (l w) k")))
        P.load(gfin, lambda q: q.dma_start(out=gfin[:], in_=fin_d.ap().partition_broadcast(128)))
        P.load(tl, lambda q: q.dma_start(out=tl[:], in_=t5_d.ap()[15:16, :].partition_broadcast(128)))
        P.load(tr, lambda q: q.dma_start(out=tr[:], in_=t5_d.ap()[31:32, :].partition_broadcast(128)))
        import os as _os
        KS = int(_os.environ.get("KSETUP", "9"))
        if KS >= 1:
            t5s = P.sb("t5s", [32, 12], F32)
            P.load(t5s, lambda q: q.dma_start(out=t5s[:], in_=t5_d.ap()))
            oh = P.sb("oh", [32, LD], F32)
            lmt = P.sb("lmt", [1, LD], F32)
            gsb = P.sb("gsb", [8, LD], F32)
            t5b = P.sb("t5b", [32, 12], BF16)
            ohb16 = P.sb("ohb16", [32, LD], BF16)
            lm16 = P.sb("lm16", [1, LD], BF16)
            P.op("dve", lambda e: e.tensor_copy(out=t5b[:], in_=t5s[:]), reads=[t5s], writes=[t5b])
            P.load(oh, lambda q: q.dma_start(out=oh[:, 0:LB], in_=ohb_d.ap()))
            P.op("dve", lambda e: e.tensor_copy(out=ohb16[:, 0:LB], in_=oh[:, 0:LB]), reads=[oh], writes=[ohb16])
            for c0 in range(0, LB, 512):
                cw = min(512, LB - c0)
                P.op("pe", lambda e, c0=c0, cw=cw: e.matmul(pb[0][0:4, 0:cw], lhsT=t5b[0:32, 0:4], rhs=ohb16[0:32, c0:c0 + cw],
                                                             start=True, stop=True), reads=[t5b, ohb16], writes=[pb[0]])
                P.op("dve", lambda e, c0=c0, cw=cw: e.tensor_copy(out=gsb[0:4, c0:c0 + cw], in_=pb[0][0:4, 0:cw]),
                     reads=[pb[0]], writes=[gsb])
            P.store(gsb, lambda q: q.dma_start(out=GB.ap(), in_=gsb[0:4, 0:LB]))
            P.load(oh, lambda q: q.dma_start(out=oh[:, 0:LD], in_=ohd_d.ap()))
            P.load(lmt, lambda q: q.dma_start(out=lmt[:], in_=lm_d.ap()))
            P.op("dve", lambda e: e.tensor_copy(out=ohb16[:, 0:LD], in_=oh[:, 0:LD]), reads=[oh], writes=[ohb16])
            P.op("dve", lambda e: e.tensor_copy(out=lm16[:], in_=lmt[:]), reads=[lmt], writes=[lm16])
            for c0 in range(0, LD, 512):
                def mm(e, c0=c0):
                    e.matmul(pb[0][0:8, :], lhsT=t5b[0:32, 4:12], rhs=ohb16[0:32, c0:c0 + 512], start=True, stop=False)
                    return e.matmul(pb[0][0:8, :], lhsT=ones16[0:1, 0:8], rhs=lm16[0:1, c0:c0 + 512], start=False, stop=True)
                P.op("pe", mm, reads=[t5b, ohb16, lm16, ones16], writes=[pb[0]])
                P.op("dve", lambda e, c0=c0: e.tensor_copy(out=gsb[0:8, c0:c0 + 512], in_=pb[0][0:8, :]),
                     reads=[pb[0]], writes=[gsb])
            P.store(gsb, lambda q: q.dma_start(out=GD.ap(), in_=gsb[0:8, :]))

        if KS >= 2:
            ld = [P.sb("wld%d" % i, [128, FF], F32) for i in range(2)]
            cv = [P.sb("wcv%d" % i, [128, FF], BF16) for i in range(2)]
            cnt = [0]

            def cast(dst_ap_fn, src_ap_fn, sc_ap, const, reads, writes):
                i = cnt[0]
                cnt[0] += 1
                if sc_ap is not None:
                    P.op("dve", lambda e: e.tensor_scalar_mul(out=dst_ap_fn(), in0=src_ap_fn(), scalar1=sc_ap), reads=reads, writes=writes)
                elif i % 2 == 0:
                    P.op("dve", lambda e: e.tensor_copy(out=dst_ap_fn(), in_=src_ap_fn()), reads=reads, writes=writes)
                else:
                    P.op("act", lambda e: e.activation(out=dst_ap_fn(), in_=src_ap_fn(), func=AF.Copy), reads=reads, writes=writes)

            it = [0]

            def slot():
                s = it[0] % 2
                it[0] += 1
                return ld[s], cv[s]

            for l in range(DEPTH):
                for w in range(2):
                    gw = 0 if w == 0 else 2
                    for m, src in enumerate((wgate_d[w], wup_d[w])):
                        for kc in range(KC):
                            a, b = slot()
                            P.load(a, lambda q, a=a, src=src, l=l, kc=kc: q.dma_start(out=a[:], in_=src.ap()[l, kc * 128:(kc + 1) * 128, :]))
                            cast(lambda b=b: b[:], lambda a=a: a[:], gain_ap(l, gw, kc), 1.0, [a, gains], [b])
                            P.store(b, lambda q, b=b, l=l, w=w, m=m, kc=kc: q.dma_start(
                                out=WGU[l][w].ap().rearrange("f p (m k j) -> f p m k j", m=2, k=KC)[:, :, m, kc, :].rearrange("f p j -> p f j"),
                                in_=b[:].rearrange("p (f j) -> p f j", j=128)))
                    for fc in range(NF):
                        a, b = slot()
                        P.load(a, lambda q, a=a, l=l, w=w, fc=fc: q.dma_start(out=a[:, 0:D], in_=wdown_d[w].ap()[l, fc * 128:(fc + 1) * 128, :]))
                        cast(lambda b=b: b[:, 0:D], lambda a=a: a[:, 0:D], None, 1.0, [a], [b])
                        P.store(b, lambda q, b=b, l=l, w=w, fc=fc: q.dma_start(out=WDN[l][w].ap()[fc * 128:(fc + 1) * 128, :], in_=b[:, 0:D]))
                for kc in range(KC):
                    a, b = slot()
                    P.load(a, lambda q, a=a, l=l, kc=kc: q.dma_start(out=a[:, 0:INW], in_=win_d.ap()[l, kc * 128:(kc + 1) * 128, :]))
                    for (dst, src, wd_, sc) in WIN_SEGS:
                        P.op("dve", lambda e, a=a, b=b, dst=dst, src=src, wd_=wd_, sc=sc, l=l, kc=kc: e.tensor_scalar(
                            out=b[:, dst:dst + wd_], in0=a[:, src:src + wd_], scalar1=gain_ap(l, 1, kc), scalar2=sc,
                            op0=ALU.mult, op1=ALU.mult), reads=[a, gains], writes=[b])
                    P.store(b, lambda q, b=b, l=l, kc=kc: q.dma_start(out=WIN[l].ap()[kc * 128:(kc + 1) * 128, :], in_=b[:, 0:INW]))
                for kc in range(KC):
                    a, b = slot()
                    P.load(a, lambda q, a=a, l=l, kc=kc: q.dma_start(out=a[:, 0:D], in_=wout_d.ap()[l, kc * 128:(kc + 1) * 128, :]))
                    cast(lambda b=b: b[:, 0:D], lambda a=a: a[:, 0:D], None, 1.0, [a], [b])
                    P.store(b, lambda q, b=b, l=l, kc=kc: q.dma_start(out=WOUT[l].ap()[kc * 128:(kc + 1) * 128, :], in_=b[:, 0:D]))

    def rstd_of(xb, xap_fn, junk, st, col):
        P.op("dve", lambda e: e.memset(st[:, col:col + 1], 0.0), writes=[st])
        P.op("act", lambda e: e.activation(out=junk[:], in_=xap_fn(), func=AF.Square, accum_out=st[:, col:col + 1]),
             reads=[xb, st], writes=[junk, st])
        P.op("act", lambda e: e.activation(out=st[:, 8 + col:9 + col], in_=st[:, col:col + 1], func=AF.Sqrt,
                                           scale=1.0 / D, bias=epst[:, 0:1]), reads=[st, epst], writes=[st])
        P.op("dve", lambda e: e.reciprocal(out=st[:, 16 + col:17 + col], in_=st[:, 8 + col:9 + col]), reads=[st], writes=[st])
        return lambda: st[:, 16 + col:17 + col]

    PS = {}

    def prep_group(g, src_dram, xs, hb, hT, junk, st):
        pst = PS["pst"]
        for tb in range(4):
            r0 = (g * 4 + tb) * 128
            P.load(xs[tb], lambda q, tb=tb, r0=r0: q.dma_start(out=xs[tb][:], in_=src_dram.ap()[r0:r0 + 128, :]))
        for tb in range(4):
            rs = rstd_of(xs[tb], lambda tb=tb: xs[tb][:], junk, st, tb)
            P.op("dve", lambda e, tb=tb, rs=rs: e.tensor_scalar_mul(out=hb[:], in0=xs[tb][:], scalar1=rs()),
                 reads=[xs[tb], st], writes=[hb])
            for half in range(2):
                def tr(e, half=half):
                    ins = None
                    for j in range(8):
                        kc = half * 8 + j
                        ins = e.transpose(pst[:, j * 128:(j + 1) * 128], hb[:, kc * 128:(kc + 1) * 128], ident[:])
                    return ins
                P.op("pe", tr, reads=[hb, ident], writes=[pst])
                eng = "dve"
                if eng == "act":
                    fn = lambda e, tb=tb, half=half: e.activation(
                        out=hT[:, half * 8:half * 8 + 8, tb * 128:(tb + 1) * 128],
                        in_=pst[:].rearrange("p (j t) -> p j t", j=8), func=AF.Copy)
                else:
                    fn = lambda e, tb=tb, half=half: e.tensor_copy(
                        out=hT[:, half * 8:half * 8 + 8, tb * 128:(tb + 1) * 128],
                        in_=pst[:].rearrange("p (j t) -> p j t", j=8))
                P.op(eng, fn, reads=[pst], writes=[hT])

    def ffn_phase(l, w, src_dram, final):
        with P.scope():
            pb, pst = std_psum()
            PS["pst"] = pst
            xs = [[P.sb("fx%d_%d" % (s, i), [128, D], F32) for i in range(4)] for s in range(2)]
            hb = P.sb("fhb", [128, D], BF16)
            hT = P.sb("fhT", [128, KC, 512], BF16)
            aT = P.sb("faT", [128, NF, 512], BF16)
            wgu = [P.sb("fwgu%d" % i, [128, 2, KC, 128], BF16) for i in range(3)]
            wdn = [P.sb("fwdn%d" % i, [128, 4, 512], BF16) for i in range(3)]
            sg = [P.sb("fsg%d" % i, [128, 512], F32) for i in range(2)]
            junk = P.sb("fjunk", [128, D], BF16)
            st = P.sb("fst", [128, 24], F32)
            gate_banks = [pb[4], pb[5], pb[6]]
            bi = [0]
            wi = [0]
            di = [0]
            gn = (0 if w == 0 else 2)

            def load_wgu(f):
                s = wgu[wi[0] % 3]
                wi[0] += 1
                P.load(s, lambda q, s=s, f=f: q.dma_start(out=s[:].rearrange("p m k j -> p (m k j)"), in_=WGU[l][w].ap()[f]))
                return s

            prep_group(0, src_dram, xs[0], hb, hT, junk, st)
            pend = load_wgu(0)
            for g in range(NGr):
                X = xs[g % 2]
                for f in range(NF):
                    ws = pend
                    if f + 1 < NF:
                        pend = load_wgu(f + 1)
                    elif g + 1 < NGr:
                        pend = load_wgu(0)
                    bg = gate_banks[bi[0] % 3]
                    bu = gate_banks[(bi[0] + 1) % 3]
                    bi[0] += 2
                    for m, bank in ((0, bg), (1, bu)):
                        def mm(e, ws=ws, m=m, bank=bank):
                            ins = None
                            for kc in range(KC):
                                ins = e.matmul(bank[:, :], lhsT=ws[:, m, kc, :], rhs=hT[:, kc, :], start=(kc == 0), stop=(kc == KC - 1))
                            return ins
                        P.op("pe", mm, reads=[ws, hT], writes=[bank])
                    sgt = sg[f % 2]
                    P.op("act", lambda e, sgt=sgt, bg=bg: e.activation(out=sgt[:], in_=bg[:, :], func=AF.Silu), reads=[bg], writes=[sgt])
                    P.op("dve", lambda e, sgt=sgt, bu=bu, f=f: e.tensor_tensor(out=aT[:, f, :], in0=bu[:, :], in1=sgt[:], op=ALU.mult),
                         reads=[bu, sgt], writes=[aT])
                if g + 1 < NGr:
                    prep_group(g + 1, src_dram, xs[(g + 1) % 2], hb, hT, junk, st)
                for n in range(4):
                    for fq in range(NF // 4):
                        s = wdn[di[0] % 3]
                        di[0] += 1
                        P.load(s, lambda q, s=s, fq=fq, n=n: q.dma_start(
                            out=s[:], in_=WDN[l][w].ap()[fq * 512:(fq + 1) * 512, n * 512:(n + 1) * 512].rearrange("(j p) c -> p j c", p=128)))

                        def mm(e, s=s, fq=fq):
                            ins = None
                            for j in range(4):
                                f = fq * 4 + j
                                for tb in range(4):
                                    ins = e.matmul(pb[tb][:, :], lhsT=aT[:, f, tb * 128:(tb + 1) * 128], rhs=s[:, j, :],
                                                   start=(f == 0), stop=(f == NF - 1))
                            return ins
                        P.op("pe", mm, reads=[s, aT], writes=[pb[0], pb[1], pb[2], pb[3]])
                    for tb in range(4):
                        P.op("dve", lambda e, tb=tb, n=n, X=X: e.scalar_tensor_tensor(
                            out=X[tb][:, n * 512:(n + 1) * 512], in0=pb[tb][:, :], scalar=0.5, in1=X[tb][:, n * 512:(n + 1) * 512],
                            op0=ALU.mult, op1=ALU.add), reads=[pb[tb], X[tb]], writes=[X[tb]])
                for tb in range(4):
                    r0 = (g * 4 + tb) * 128
                    if final:
                        rs = rstd_of(X[tb], lambda tb=tb, X=X: X[tb][:], junk, st, 4 + tb % 2)
                        P.op("dve", lambda e, tb=tb, X=X, rs=rs: e.scalar_tensor_tensor(
                            out=X[tb][:], in0=X[tb][:], scalar=rs(), in1=gfin[:], op0=ALU.mult, op1=ALU.mult),
                            reads=[X[tb], st, gfin], writes=[X[tb]])
                        P.store(X[tb], lambda q, tb=tb, X=X, r0=r0: q.dma_start(out=y_d.ap()[r0:r0 + 128, :], in_=X[tb][:]))
                    else:
                        P.store(X[tb], lambda q, tb=tb, X=X, r0=r0: q.dma_start(out=xres.ap()[r0:r0 + 128, :], in_=X[tb][:]))

    def proj_phase(l):
        with P.scope():
            pb, pst = std_psum()
            PS["pst"] = pst
            xs = [P.sb("px%d" % i, [128, D], F32) for i in range(4)]
            hb = P.sb("phb", [128, D], BF16)
            hT = P.sb("phT", [128, KC, 512], BF16)
            junk = P.sb("pjunk", [128, D], BF16)
            st = P.sb("pst_", [128, 24], F32)
            wch = [P.sb("pw%d" % i, [128, KC, 512], BF16) for i in range(2)]
            qk32 = P.sb("pqk32", [128, 4, 640], F32)
            tmp32 = P.sb("ptmp32", [128, 640], F32)
            tmp2 = P.sb("ptmp2", [128, 640], F32)
            qka = P.sb("pqka", [128, 4, 640], BF16)
            qkb = P.sb("pqkb", [128, 4, 6, 512], BF16)
            vb16 = P.sb("pvb", [128, 4, 3, 512], BF16)
            va16 = P.sb("pva", [128, 4, 128], BF16)
            cs = P.sb("pcs", [128, 4, 64], F32)
            gq8 = P.sb("pgq8", [128, 64], F32)
            gk = P.sb("pgk", [128, 64], F32)
            n10 = P.sb("pn10", [128, 32], F32)
            stg = [P.sb("pstg%d" % i, [128, 512], BF16) for i in range(4)]
            P.load(gq8, lambda q: q.dma_start(out=gq8[:], in_=aq_d.ap()[l:l + 1, :].partition_broadcast(128)))
            P.load(gk, lambda q: q.dma_start(out=gk[:], in_=ak_d.ap()[l:l + 1, :].partition_broadcast(128)))
            P.op("dve", lambda e: e.tensor_scalar_mul(out=gq8[:], in0=gq8[:], scalar1=0.125), reads=[gq8], writes=[gq8])
            wi = [0]
            si = [0]
            ei = [0]
            for g in range(NGr):
                prep_group(g, xres, xs, hb, hT, junk, st)
                P.load(cs, lambda q, g=g: q.dma_start(out=cs[:, :, 0:32], in_=cos_d.ap()[g * 512:(g + 1) * 512, :].rearrange("(t p) c -> p t c", p=128)))
                P.load(cs, lambda q, g=g: q.dma_start(out=cs[:, :, 32:64], in_=sin_d.ap()[g * 512:(g + 1) * 512, :].rearrange("(t p) c -> p t c", p=128)))
                for c in range(11):
                    cw = 512 if c < 10 else 256
                    ws = wch[wi[0] % 2]
                    wi[0] += 1
                    P.load(ws, lambda q, ws=ws, c=c, cw=cw: q.dma_start(
                        out=ws[:, :, 0:cw], in_=WIN[l].ap()[:, c * 512:c * 512 + cw].rearrange("(k p) c -> p k c", p=128)))
                    for tb in range(4):
                        bank = pb[(c * 4 + tb) % 4]

                        def mm(e, ws=ws, tb=tb, bank=bank, cw=cw):
                            ins = None
                            for kc in range(KC):
                                ins = e.matmul(bank[:, 0:cw], lhsT=hT[:, kc, tb * 128:(tb + 1) * 128], rhs=ws[:, kc, 0:cw],
                                               start=(kc == 0), stop=(kc == KC - 1))
                            return ins
                        P.op("pe", mm, reads=[ws, hT], writes=[bank])
                        eng = "dve"

                        def cp(eng, out_fn, in_fn, reads, writes):
                            if eng == "act":
                                P.op("act", lambda e: e.activation(out=out_fn(), in_=in_fn(), func=AF.Copy), reads=reads, writes=writes)
                            else:
                                P.op("dve", lambda e: e.tensor_copy(out=out_fn(), in_=in_fn()), reads=reads, writes=writes)
                        if c == 0:
                            cp(eng, lambda tb=tb: qk32[:, tb, 0:512], lambda bank=bank: bank[:, :], [bank], [qk32])
                        elif c <= 6:
                            cp(eng, lambda tb=tb, c=c: qkb[:, tb, c - 1, :], lambda bank=bank: bank[:, :], [bank], [qkb])
                        elif c <= 9:
                            cp(eng, lambda tb=tb, c=c: vb16[:, tb, c - 7, :], lambda bank=bank: bank[:, :], [bank], [vb16])
                        else:
                            cp("dve", lambda tb=tb: qk32[:, tb, 512:640], lambda bank=bank: bank[:, 0:128], [bank], [qk32])
                            cp("dve", lambda tb=tb: va16[:, tb, :], lambda bank=bank: bank[:, 128:256], [bank], [va16])
                P.store(va16, lambda q, g=g: q.dma_start(out=VA.ap()[g * 512:(g + 1) * 512, :].rearrange("(t p) c -> p t c", p=128), in_=va16[:]))
                for m in range(3):
                    P.store(vb16, lambda q, g=g, m=m: q.dma_start(out=VX[m].ap()[g * 512:(g + 1) * 512, :].rearrange("(t p) c -> p t c", p=128), in_=vb16[:, :, m, :]))
                for tb in range(4):
                    x3f = lambda tb=tb: qk32[:, tb, :].rearrange("p (h d) -> p h d", d=64)
                    t3f = lambda: tmp32[:].rearrange("p (h d) -> p h d", d=64)
                    a4f = lambda: tmp32[:].rearrange("p (h i two) -> p h i two", h=10, two=2)
                    j4f = lambda: tmp2[:].rearrange("p (h i two) -> p h i two", h=10, two=2)
                    o4f = lambda tb=tb: qka[:, tb, :].rearrange("p (h i two) -> p h i two", h=10, two=2)
                    ccf = lambda tb=tb: cs[:, tb, 0:32].unsqueeze(1).to_broadcast([128, 10, 32])
                    ssf = lambda tb=tb: cs[:, tb, 32:64].unsqueeze(1).to_broadcast([128, 10, 32])
                    P.op("dve", lambda e, tb=tb: e.tensor_tensor(out=tmp32[:], in0=qk32[:, tb, :], in1=qk32[:, tb, :], op=ALU.mult), reads=[qk32], writes=[tmp32])
                    P.op("dve", lambda e, t3f=t3f: e.reduce_sum(out=n10[:, 0:10], in_=t3f(), axis=AX.X), reads=[tmp32], writes=[n10])
                    P.op("act", lambda e: e.activation(out=n10[:, 10:20], in_=n10[:, 0:10], func=AF.Sqrt, scale=1.0 / 64, bias=epst[:, 0:1]),
                         reads=[n10, epst], writes=[n10])
                    P.op("dve", lambda e: e.reciprocal(out=n10[:, 20:30], in_=n10[:, 10:20]), reads=[n10], writes=[n10])
                    P.op("dve", lambda e, x3f=x3f, t3f=t3f: e.tensor_tensor(out=t3f(), in0=x3f(), in1=n10[:, 20:30].unsqueeze(2).to_broadcast([128, 10, 64]), op=ALU.mult),
                         reads=[qk32, n10], writes=[tmp32])
                    P.op("dve", lambda e, t3f=t3f: e.tensor_tensor(out=t3f()[:, 0:8, :], in0=t3f()[:, 0:8, :], in1=gq8[:].unsqueeze(1).to_broadcast([128, 8, 64]), op=ALU.mult),
                         reads=[tmp32, gq8], writes=[tmp32])
                    P.op("dve", lambda e, t3f=t3f: e.tensor_tensor(out=t3f()[:, 8:10, :], in0=t3f()[:, 8:10, :], in1=gk[:].unsqueeze(1).to_broadcast([128, 2, 64]), op=ALU.mult),
                         reads=[tmp32, gk], writes=[tmp32])
                    P.op("dve", lambda e, a4f=a4f, j4f=j4f, ccf=ccf: e.tensor_tensor(out=j4f()[:, :, :, 0], in0=a4f()[:, :, :, 0], in1=ccf(), op=ALU.mult), reads=[tmp32, cs], writes=[tmp2])
                    P.op("dve", lambda e, a4f=a4f, j4f=j4f, ssf=ssf: e.tensor_tensor(out=j4f()[:, :, :, 1], in0=a4f()[:, :, :, 1], in1=ssf(), op=ALU.mult), reads=[tmp32, cs], writes=[tmp2])
                    P.op("dve", lambda e, o4f=o4f, j4f=j4f: e.tensor_tensor(out=o4f()[:, :, :, 0], in0=j4f()[:, :, :, 0], in1=j4f()[:, :, :, 1], op=ALU.subtract), reads=[tmp2], writes=[qka])
                    P.op("dve", lambda e, a4f=a4f, j4f=j4f, ssf=ssf: e.tensor_tensor(out=j4f()[:, :, :, 0], in0=a4f()[:, :, :, 0], in1=ssf(), op=ALU.mult), reads=[tmp32, cs], writes=[tmp2])
                    P.op("dve", lambda e, a4f=a4f, j4f=j4f, ccf=ccf: e.tensor_tensor(out=j4f()[:, :, :, 1], in0=a4f()[:, :, :, 1], in1=ccf(), op=ALU.mult), reads=[tmp32, cs], writes=[tmp2])
                    P.op("dve", lambda e, o4f=o4f, j4f=j4f: e.tensor_tensor(out=o4f()[:, :, :, 1], in0=j4f()[:, :, :, 0], in1=j4f()[:, :, :, 1], op=ALU.add), reads=[tmp2], writes=[qka])
                chunks = []
                for i in range(5):
                    chunks.append(("a", i))
                for c in range(6):
                    for j in range(4):
                        chunks.append((c, j))
                for (kind, j) in chunks:
                    def tr(e, kind=kind, j=j):
                        ins = None
                        for tb in range(4):
                            src = qka[:, tb, j * 128:(j + 1) * 128] if kind == "a" else qkb[:, tb, kind, j * 128:(j + 1) * 128]
                            ins = e.transpose(pst[:, tb * 128:(tb + 1) * 128], src, ident[:])
                        return ins
                    P.op("pe", tr, reads=[qka if kind == "a" else qkb, ident], writes=[pst])
                    sg_ = stg[si[0] % 4]
                    si[0] += 1
                    if False:
                        P.op("act", lambda e, sg_=sg_: e.activation(out=sg_[:], in_=pst[:, 0:512], func=AF.Copy), reads=[pst], writes=[sg_])
                    else:
                        P.op("dve", lambda e, sg_=sg_: e.tensor_copy(out=sg_[:], in_=pst[:, 0:512]), reads=[pst], writes=[sg_])
                    for half in range(2):
                        hd = 2 * j + half
                        rows = slice(half * 64, half * 64 + 64)
                        if kind == "a" and j < 4:
                            dst = QTA.ap()[hd // 4].rearrange("d (b h q) -> d b h q", h=4, q=128)[:, g * 4:g * 4 + 4, hd % 4, :]
                            P.store(sg_, lambda q, sg_=sg_, rows=rows, dst=dst: q.dma_start(out=dst, in_=sg_[rows, :].rearrange("d (b q) -> d b q", q=128)))
                            continue
                        if kind == "a":
                            dst = KTA.ap()[half][:, g * 512:(g + 1) * 512]
                        elif kind < 3:
                            dst = QTX[kind].ap()[hd][:, g * 512:(g + 1) * 512]
                        else:
                            dst = KTX[kind - 3].ap()[hd][:, g * 512:(g + 1) * 512]
                        P.store(sg_, lambda q, sg_=sg_, rows=rows, dst=dst: q.dma_start(out=dst, in_=sg_[rows, :]))

    def att_phase(l):
        with P.scope():
            SG = [P.ps("sg0", [128, 1536]), P.ps("sg1", [128, 1536])]
            obank = P.ps("obank", [128, 512])
            misc = P.ps("misc", [128, 512])
            P.pe_tok = misc
            kt = P.sb("akt", [128, T], BF16)
            vt = P.sb("avt", [128, NB, 130], BF16)
            qt = [P.sb("aqt%d" % i, [128, 512], BF16) for i in range(2)]
            pt = [P.sb("apt%d" % i, [128, 1536], BF16) for i in range(2)]
            osb = P.sb("aosb", [128, 512], F32)
            rz = P.sb("arz", [128, 512], F32)
            o16 = [P.sb("ao16_%d" % i, [128, 512], BF16) for i in range(2)]
            zs = [P.sb("azs%d" % i, [128, 512], BF16) for i in range(2)]
            wt32 = P.sb("awt32", [128, WD_W], F32)
            wt = P.sb("awt", [128, WD_W], BF16)
            rpb32 = P.sb("arpb", [128, 8, 512], F32)
            mk = P.sb("amk", [128, 8, 512], F32)
            cb16 = P.sb("acb", [128, 8, 512], BF16)
            kmt = P.sb("akmt", [128, 4096], F32)
            qi = [0]
            oi = [0]
            gi = [0]
            for c in range(4):
                P.load(kmt, lambda q, c=c: q.dma_start(out=kmt[64:65, :], in_=kmrow_d.ap()[:, c * 4096:(c + 1) * 4096]))
                P.op("dve", lambda e, c=c: e.tensor_copy(out=kt[64:65, c * 4096:(c + 1) * 4096], in_=kmt[64:65, :]), reads=[kmt], writes=[kt])
            for i in range(2):
                P.op("dve", lambda e, i=i: e.memset(qt[i][64:65, :], 1.0), reads=[], writes=[qt[i]])

            def bias_ap(bk):
                if bk is None:
                    return zt[:, 0:1]
                side, h = bk
                return tl[:, h:h + 1] if side == "l" else tr[:, h:h + 1]

            def flash(units, dv, zsep, store_fn):
                groups = []
                for ui, (qsrc, pairs) in enumerate(units):
                    gl = []
                    cur = []
                    for pr in pairs:
                        if cur and (len(cur) == 3 or cur[0][2] != pr[2]):
                            gl.append(cur)
                            cur = []
                        cur.append(pr)
                    if cur:
                        gl.append(cur)
                    for k, gpr in enumerate(gl):
                        groups.append((ui, k == 0, k == len(gl) - 1, gpr))
                qbufs = {}

                def qk(idx):
                    ui, first, last, gpr = groups[idx]
                    if first:
                        qb_ = qt[qi[0] % 2]
                        qi[0] += 1
                        qsrc = units[ui][0]
                        P.load(qb_, lambda q, qb_=qb_, qsrc=qsrc: q.dma_start(out=qb_[0:64, :], in_=qsrc))
                        qbufs[ui] = qb_
                    qb_ = qbufs[ui]
                    sg_ = SG[(gi[0] + idx) % 2]

                    def mm(e, gpr=gpr, sg_=sg_, qb_=qb_):
                        ins = None
                        for j, (kb, brhs, bk) in enumerate(gpr):
                            ins = e.matmul(sg_[:, j * 512:(j + 1) * 512], lhsT=kt[0:65, kb * 128:(kb + 1) * 128], rhs=qb_[0:65, :],
                                           start=True, stop=(brhs is None))
                            if brhs is not None:
                                ins = e.matmul(sg_[:, j * 512:(j + 1) * 512], lhsT=jm[:], rhs=brhs(), start=False, stop=True)
                        return ins
                    rd = [kt, qb_]
                    if gpr[0][1] is not None:
                        rd += [jm, wt, cb16]
                    P.op("pe", mm, reads=rd, writes=[sg_])

                qk(0)
                for idx in range(len(groups)):
                    ui, first, last, gpr = groups[idx]
                    if idx + 1 < len(groups):
                        qk(idx + 1)
                    n = len(gpr)
                    sg_ = SG[(gi[0] + idx) % 2]
                    p_ = pt[(gi[0] + idx) % 2]
                    bk = gpr[0][2]
                    P.op("act", lambda e, sg_=sg_, p_=p_, bk=bk, n=n: e.activation(out=p_[:, 0:n * 512], in_=sg_[:, 0:n * 512], func=AF.Exp,
                                                                             bias=bias_ap(bk), scale=1.0),
                         reads=[sg_, tl, tr, zt], writes=[p_])

                    zrhs = None
                    z_ = None
                    if zsep:
                        if n == 1:
                            zrhs = lambda p_=p_: p_[:, 0:512]
                        else:
                            z_ = zs[(gi[0] + idx) % 2]
                            P.op("dve", lambda e, z_=z_, p_=p_: e.tensor_tensor(out=z_[:], in0=p_[:, 0:512], in1=p_[:, 512:1024], op=ALU.add),
                                 reads=[p_], writes=[z_])
                            if n == 3:
                                P.op("dve", lambda e, z_=z_, p_=p_: e.tensor_tensor(out=z_[:], in0=z_[:], in1=p_[:, 1024:1536], op=ALU.add),
                                     reads=[p_, z_], writes=[z_])
                            zrhs = lambda z_=z_: z_[:]

                    def pv(e, gpr=gpr, p_=p_, first=first, last=last, n=n):
                        ins = None
                        for j, (kb, brhs, bk_) in enumerate(gpr):
                            st_ = first and j == 0
                            sp_ = last and j == n - 1
                            if zsep:
                                ins = e.matmul(obank[0:dv, :], lhsT=vt[:, kb, 0:dv], rhs=p_[:, j * 512:(j + 1) * 512], start=st_, stop=sp_)
                            else:
                                ins = e.matmul(obank[0:dv + 1, :], lhsT=vt[:, kb, 0:dv + 1], rhs=p_[:, j * 512:(j + 1) * 512], start=st_, stop=sp_)
                        return ins
                    P.op("pe", pv, reads=[vt, p_], writes=[obank])
                    if zsep:
                        P.op("pe", lambda e, zrhs=zrhs, first=first, last=last: e.matmul(misc[0:1, :], lhsT=ones16[:, 0:1], rhs=zrhs(), start=first, stop=last),
                             reads=[ones16, p_] + ([z_] if z_ is not None else []), writes=[misc])
                    if last:
                        zr = 0 if zsep else dv
                        zsrc = misc if zsep else obank

                        P.op("dve", lambda e, zsrc=zsrc, zr=zr: e.tensor_scalar_max(out=rz[zr:zr + 1, :], in0=zsrc[zr:zr + 1, :], scalar1=1e-30),
                             reads=[zsrc], writes=[rz])
                        P.op("dve", lambda e, zr=zr: e.reciprocal(out=rz[zr:zr + 1, :], in_=rz[zr:zr + 1, :]), reads=[rz], writes=[rz])
                        P.op("act", lambda e: e.activation(out=osb[0:dv, :], in_=obank[0:dv, :], func=AF.Copy), reads=[obank], writes=[osb])
                        P.op("pe", lambda e, zr=zr: e.matmul(misc[0:dv, :], lhsT=ones32[zr:zr + 1, 0:dv], rhs=rz[zr:zr + 1, :], start=True, stop=True),
                             reads=[rz, ones32], writes=[misc])
                        o_ = o16[oi[0] % 2]
                        oi[0] += 1
                        P.op("dve", lambda e, o_=o_: e.tensor_tensor(out=o_[0:dv, :], in0=osb[0:dv, :], in1=misc[0:dv, :], op=ALU.mult),
                             reads=[osb, misc], writes=[o_])
                        store_fn(ui, o_)
                gi[0] += len(groups)

            def load_k(src_ap):
                for c in range(4):
                    P.load(kt, lambda q, c=c: q.dma_start(out=kt[0:64, c * 4096:(c + 1) * 4096], in_=src_ap[:, c * 4096:(c + 1) * 4096]), nowaw=(c > 0))

            def load_v(src_ap, dv):
                for c in range(4):
                    P.load(vt, lambda q, c=c: q.dma_start(out=vt[:, c * 32:(c + 1) * 32, 0:dv],
                                                          in_=src_ap[c * 4096:(c + 1) * 4096, :].rearrange("(b p) d -> p b d", p=128)), nowaw=(c > 0))

            def load_w(off, width, tensor):
                P.load(wt32, lambda q: q.dma_start(out=wt32[:, 0:width], in_=bass.AP(tensor, off, [[1, 128], [1, width]])))
                P.op("dve", lambda e: e.tensor_copy(out=wt[:, 0:width], in_=wt32[:, 0:width]), reads=[wt32], writes=[wt])

            for gk_ in lim(range(2)):
                load_k(KTA.ap()[gk_])
                load_v(VA.ap()[:, gk_ * 64:(gk_ + 1) * 64], 64)
                P.op("dve", lambda e: e.memset(vt[:, :, 64:65], 1.0), reads=[], writes=[vt])
                units = []
                for qb in range(NB):
                    qsrc = QTA.ap()[gk_][:, qb * 512:(qb + 1) * 512]
                    units.append((qsrc, [(kb, None, None) for kb in range(NB)]))

                def st_a(ui, o_, gk_=gk_):
                    dst = ATT.ap()[gk_ * 256:(gk_ + 1) * 256, ui * 128:(ui + 1) * 128].rearrange("(h d) q -> d h q", d=64)
                    P.store(o_, lambda q, o_=o_, dst=dst: q.dma_start(out=dst, in_=o_[0:64, :].rearrange("d (h q) -> d h q", q=128)))
                flash(lim(units), 64, False, st_a)
            for hm in lim(range(8)):
                h = hm // 2
                load_k(KTX[0].ap()[hm])
                load_v(VX[0].ap()[:, h * 128:(h + 1) * 128], 128)
                load_w(h * LB, WB_W, GB)
                units = []
                for gq in range(NG):
                    qsrc = QTX[0].ap()[hm][:, gq * 512:(gq + 1) * 512]
                    pairs = []
                    for kb in range(NB):
                        dl = kb - 4 * gq
                        if -5 <= dl <= 8:
                            s_ = 1024 - 128 * dl
                            pairs.append((kb, (lambda s_=s_: wt[:, s_:s_ + 512]), None))
                        elif dl < 0:
                            pairs.append((kb, None, ("l", h)))
                        else:
                            pairs.append((kb, None, ("r", h)))
                    units.append((qsrc, pairs))

                def st_b(ui, o_, hm=hm):
                    dst = OB.ap()[hm][:, ui * 512:(ui + 1) * 512]
                    P.store(o_, lambda q, o_=o_, dst=dst: q.dma_start(out=dst, in_=o_[:, :]))
                flash(lim(units), 128, True, st_b)
            for h in lim(range(8)):
                load_k(KTX[1].ap()[h])
                load_v(VX[1].ap()[:, h * 64:(h + 1) * 64], 64)
                P.op("dve", lambda e: e.memset(vt[:, :, 64:65], 1.0), reads=[], writes=[vt])
                P.op("dve", lambda e: e.memset(rpb32[:], NEG), reads=[], writes=[rpb32])
                for dl in range(-2, 6):
                    for krl in range(2):
                        for qrl in range(8):
                            rr = 2 * dl + krl - qrl + 7
                            if rr < 0 or rr > 14:
                                continue
                            off = ((l * 8 + h) * 15 + rr) * 127
                            P.load(rpb32, lambda q, dl=dl, krl=krl, qrl=qrl, off=off: q.dma_start(
                                out=rpb32[(1 - krl) * 64:(1 - krl) * 64 + 64, dl + 2, qrl * 64:qrl * 64 + 64],
                                in_=bass.AP(rrev_d, off, [[1, 64], [1, 64]])), nowaw=True)
                for gq in lim(range(NG)):
                    qsrc = QTX[1].ap()[h][:, gq * 512:(gq + 1) * 512]
                    pairs = []
                    for dl in range(-2, 6):
                        kb = 4 * gq + dl
                        if kb < 0 or kb >= NB:
                            continue
                        pairs.append((kb, (lambda dl=dl: cb16[:, dl + 2, :]), None))
                    P.load(mk, lambda q, gq=gq: q.dma_start(out=mk[:].rearrange("p a b -> p (a b)"), in_=maskc_d.ap()[gq]))
                    P.op("dve", lambda e: e.tensor_tensor(out=cb16[:], in0=rpb32[:], in1=mk[:], op=ALU.add), reads=[rpb32, mk], writes=[cb16])

                    def st_c(ui, o_, gq=gq, h=h):
                        dst = ATT.ap()[1024 + h * 64:1024 + (h + 1) * 64, gq * 512:(gq + 1) * 512]
                        P.store(o_, lambda q, o_=o_, dst=dst: q.dma_start(out=dst, in_=o_[0:64, :]))
                    flash([(qsrc, pairs)], 64, False, st_c)
            for h in lim(range(8)):
                load_k(KTX[2].ap()[h])
                load_v(VX[2].ap()[:, h * 64:(h + 1) * 64], 64)
                P.op("dve", lambda e: e.memset(vt[:, :, 64:65], 1.0), reads=[], writes=[vt])
                load_w(h * LD, WD_W, GD)
                units = []
                for gq in range(NG):
                    qsrc = QTX[2].ap()[h][:, gq * 512:(gq + 1) * 512]
                    pairs = []
                    for dl in range(-8, 12):
                        kb = 4 * gq + dl
                        if kb < 0 or kb >= NB:
                            continue
                        s_ = 1408 - 128 * dl
                        pairs.append((kb, (lambda s_=s_: wt[:, s_:s_ + 512]), None))
                    units.append((qsrc, pairs))

                def st_d(ui, o_, h=h):
                    dst = ATT.ap()[1536 + h * 64:1536 + (h + 1) * 64, ui * 512:(ui + 1) * 512]
                    P.store(o_, lambda q, o_=o_, dst=dst: q.dma_start(out=dst, in_=o_[0:64, :]))
                flash(lim(units), 64, False, st_d)

    def wout_phase(l):
        lam_init = 0.8 - 0.6 * math.exp(-0.3 * l)
        with P.scope():
            pb, pst = std_psum()
            xs1 = [P.sb("ox%d" % i, [128, D], F32) for i in range(4)]
            xs = [xs1, xs1]
            mixT = [P.sb("omix%d" % i, [128, KC, 512], BF16) for i in range(2)]
            o12 = [P.sb("oo12_%d" % i, [128, 2, 4, 512], BF16) for i in range(2)]
            wo = P.sb("owo", [128, KC, D], BF16)
            d32 = P.sb("od32", [128, 512], F32)
            sq = P.sb("osq", [128, 512], F32)
            rt = P.sb("ort", [128, 512], F32)
            bl = P.sb("obl", [128, 256], F32)
            lam = P.sb("olam", [128, 8], F32)
            gsub = P.sb("ogsub", [128, 1], F32)
            P.load(wo, lambda q: q.dma_start(out=wo[:], in_=WOUT[l].ap().rearrange("(k p) c -> p k c", p=128)))
            P.load(bl, lambda q: q.dma_start(out=bl[:], in_=bl_d.ap()[l:l + 1, :].partition_broadcast(128)))
            P.load(gsub, lambda q: q.dma_start(out=gsub[:], in_=bs_d.ap()[l]))

            P.op("dve", lambda e: e.tensor_tensor(out=bl[:, 0:64], in0=bl[:, 0:64], in1=bl[:, 64:128], op=ALU.mult), reads=[bl], writes=[bl])
            P.op("dve", lambda e: e.tensor_tensor(out=bl[:, 128:192], in0=bl[:, 128:192], in1=bl[:, 192:256], op=ALU.mult), reads=[bl], writes=[bl])
            P.op("dve", lambda e: e.reduce_sum(out=lam[:, 0:1], in_=bl[:, 0:64], axis=AX.X), reads=[bl], writes=[lam])
            P.op("dve", lambda e: e.reduce_sum(out=lam[:, 1:2], in_=bl[:, 128:192], axis=AX.X), reads=[bl], writes=[lam])
            P.op("act", lambda e: e.activation(out=lam[:, 2:4], in_=lam[:, 0:2], func=AF.Exp), reads=[lam], writes=[lam])
            P.op("dve", lambda e: e.tensor_tensor(out=lam[:, 4:5], in0=lam[:, 3:4], in1=lam[:, 2:3], op=ALU.subtract), reads=[lam], writes=[lam])
            P.op("dve", lambda e: e.tensor_scalar_add(out=lam[:, 5:6], in0=lam[:, 4:5], scalar1=-lam_init), reads=[lam], writes=[lam])
            P.op("dve", lambda e: e.tensor_scalar_mul(out=gsub[:], in0=gsub[:], scalar1=(1.0 - lam_init)), reads=[gsub], writes=[gsub])

            def loads_x(g):
                X = xs[g % 2]
                for tb in range(4):
                    r0 = (g * 4 + tb) * 128
                    P.load(X[tb], lambda q, tb=tb, r0=r0, X=X: q.dma_start(out=X[tb][:], in_=xres.ap()[r0:r0 + 128, :]))

            def loads(g):
                M = mixT[g % 2]
                O = o12[g % 2]
                for (c0, r0) in ((0, 0), (8, 1024), (12, 1536)):
                    P.load(M, lambda q, c0=c0, r0=r0, M=M, g=g: q.dma_start(
                        out=M[:, c0:c0 + 4, :], in_=ATT.ap()[r0:r0 + 512, g * 512:(g + 1) * 512].rearrange("(c p) t -> p c t", p=128)))
                for m in range(2):
                    P.load(O, lambda q, m=m, O=O, g=g: q.dma_start(
                        out=O[:, m, :, :], in_=OB.ap().rearrange("(h m) p t -> m p h t", m=2)[m][:, :, g * 512:(g + 1) * 512]))

            loads(0)
            for g in range(NGr):
                X = xs[g % 2]
                M = mixT[g % 2]
                O = o12[g % 2]
                if g + 1 < NGr:
                    loads(g + 1)
                loads_x(g)
                for h in range(4):
                    P.op("dve", lambda e, h=h, O=O: e.scalar_tensor_tensor(out=d32[:], in0=O[:, 1, h, :], scalar=lam[:, 5:6], in1=O[:, 0, h, :],
                                                                         op0=ALU.mult, op1=ALU.add), reads=[O, lam], writes=[d32])
                    P.op("act", lambda e: e.activation(out=sq[:], in_=d32[:], func=AF.Square), reads=[d32], writes=[sq])
                    P.op("pe", lambda e: e.matmul(pb[4][:, :], lhsT=ones32[:, :], rhs=sq[:], start=True, stop=True), reads=[ones32, sq], writes=[pb[4]])
                    P.op("act", lambda e: e.activation(out=rt[:], in_=pb[4][:, :], func=AF.Sqrt, scale=1.0 / 128, bias=epst[:, 0:1]),
                         reads=[pb[4], epst], writes=[rt])
                    P.op("dve", lambda e: e.reciprocal(out=rt[:], in_=rt[:]), reads=[rt], writes=[rt])
                    P.op("dve", lambda e, h=h, M=M: e.scalar_tensor_tensor(out=M[:, 4 + h, :], in0=d32[:], scalar=gsub[:, 0:1], in1=rt[:],
                                                                         op0=ALU.mult, op1=ALU.mult), reads=[d32, gsub, rt], writes=[M])
                for tb in range(4):
                    for n in range(4):
                        bank = pb[(tb * 4 + n) % 4]

                        def mm(e, tb=tb, n=n, bank=bank, M=M):
                            ins = None
                            for c in range(KC):
                                ins = e.matmul(bank[:, :], lhsT=M[:, c, tb * 128:(tb + 1) * 128], rhs=wo[:, c, n * 512:(n + 1) * 512],
                                               start=(c == 0), stop=(c == KC - 1))
                            return ins
                        P.op("pe", mm, reads=[M, wo], writes=[bank])
                        P.op("dve", lambda e, tb=tb, n=n, bank=bank, X=X: e.tensor_tensor(
                            out=X[tb][:, n * 512:(n + 1) * 512], in0=bank[:, :], in1=X[tb][:, n * 512:(n + 1) * 512], op=ALU.add),
                            reads=[bank, X[tb]], writes=[X[tb]])
                    r0 = (g * 4 + tb) * 128
                    P.store(X[tb], lambda q, tb=tb, X=X, r0=r0: q.dma_start(out=xres.ap()[r0:r0 + 128, :], in_=X[tb][:]))

    import os
    PH = os.environ.get("KPH", "ffn1,proj,att,wout,ffn2").split(",")
    for l in range(int(os.environ.get("KDEPTH", DEPTH))):
        if "ffn1" in PH:
            ffn_phase(l, 0, x_d if l == 0 else xres, False)
        if "proj" in PH:
            proj_phase(l)
        if "att" in PH:
            att_phase(l)
        if "wout" in PH:
            wout_phase(l)
        if "ffn2" in PH:
            ffn_phase(l, 1, xres, l == DEPTH - 1)
    P.barrier()
    P.emit()
    P.close()
    return nc


def _t5_bucket_np(rel):
    nb = 16
    max_exact = 8
    side = (rel > 0).astype(np.int32) * nb
    n = np.abs(rel)
    nf = np.maximum(n, 1).astype(np.float32) / np.float32(max_exact)
    large = max_exact + (np.log(nf).astype(np.float32) / np.float32(math.log(1024 / max_exact)) * np.float32(nb - max_exact)).astype(np.int32)
    large = np.minimum(large, nb - 1)
    return side + np.where(n < max_exact, n, large)


def _consts(t_real):
    c = {}
    c["ident"] = np.eye(128, dtype=np.float32)
    c["jmat"] = np.ascontiguousarray(np.eye(128, dtype=np.float32)[::-1])
    t = np.arange(T, dtype=np.int32)
    inv_freq = (10000.0 ** (-np.arange(16, dtype=np.float32) / 16)).astype(np.float32)
    ang = np.concatenate([(t // 64).astype(np.float32)[:, None] * inv_freq[None, :],
                          (t % 64).astype(np.float32)[:, None] * inv_freq[None, :]], axis=-1).astype(np.float32)
    c["cos_t"] = np.cos(ang).astype(np.float32)
    c["sin_t"] = np.sin(ang).astype(np.float32)
    relb = CB - np.arange(LB)
    bb = _t5_bucket_np(relb)
    ohb = np.zeros((32, LB), np.float32)
    ohb[bb, np.arange(LB)] = 1.0
    c["ohb"] = ohb
    reld = CD - np.arange(LD)
    bd = _t5_bucket_np(reld)
    a = np.abs(reld)
    mult = (a <= 64).astype(np.int32) + ((a <= 256) & (reld % 4 == 0)).astype(np.int32) + ((a <= 1024) & (reld % 16 == 0)).astype(np.int32)
    ohd = np.zeros((32, LD), np.float32)
    ohd[bd, np.arange(LD)] = 1.0
    ohd[:, mult == 0] = 0.0
    c["ohd"] = ohd
    lm = np.where(mult > 0, np.log(np.maximum(mult, 1).astype(np.float32)), np.float32(NEG)).astype(np.float32)
    c["lm"] = lm[None, :]
    kmr_ = np.zeros((1, T), np.float32)
    kmr_[0, t_real:] = NEG
    c["kmrow"] = kmr_
    rows_real = t_real // 64
    gq = np.arange(NG)[:, None, None, None]
    pp = np.arange(128)[None, :, None, None]
    di = np.arange(8)[None, None, :, None]
    ql = np.arange(512)[None, None, None, :]
    kl = 127 - pp
    krl, kc = kl // 64, kl % 64
    qrl, qc = ql // 64, ql % 64
    kb = 4 * gq + (di - 2)
    kr = 2 * kb + krl
    r = 8 * gq + qrl
    rows_eff = np.where(r < rows_real, rows_real, T // 64)
    rs = np.clip(r - 4, 0, rows_eff - 8)
    cs_ = np.clip(qc - 8, 0, 48)
    valid = (kr >= rs) & (kr < rs + 8) & (kc >= cs_) & (kc < cs_ + 16) & (kb >= 0) & (kb < NB)
    c["maskc"] = np.where(valid, np.float32(0.0), np.float32(NEG)).astype(np.float32).reshape(NG, 128, 8 * 512)
    return c


_NC_CACHE = {}


def kernel(**inp):
    f = lambda a: np.ascontiguousarray(np.asarray(a, dtype=np.float32))
    xp = f(inp["x_prompt"])
    xsm = f(inp["x_sample"])
    seqs = [xp[0], xp[1], xsm[0], xsm[1]]
    treal = [xp.shape[1], xp.shape[1], xsm.shape[1], xsm.shape[1]]
    shared = {}
    for k in ("ffn1_w_gate", "ffn1_w_up", "ffn1_w_down", "ffn2_w_gate", "ffn2_w_up", "ffn2_w_down", "w_in", "w_out",
              "a_q_norm", "a_k_norm", "t5_table"):
        shared[k] = f(inp[k])
    g3 = np.stack([f(inp["ffn1_norm"]), f(inp["mix_norm"]), f(inp["ffn2_norm"])], axis=1)
    shared["gains_t"] = np.ascontiguousarray(g3.reshape(DEPTH, 3, KC, 128).transpose(0, 1, 3, 2))
    shared["final_norm"] = f(inp["final_norm"]).reshape(1, D)
    shared["b_lambda"] = f(inp["b_lambda"]).reshape(DEPTH, 256)
    shared["b_subln_t"] = f(inp["b_subln"]).reshape(DEPTH, 128, 1)
    rpb = f(inp["c_rpb"])
    rrev = np.full((DEPTH, 8, 15, 127), NEG, np.float32)
    rrev[..., 48:79] = rpb[..., ::-1]
    shared["rrev"] = rrev.reshape(DEPTH * 8 * 15, 127)
    cc = {t_: _consts(t_) for t_ in set(treal)}
    in_maps = []
    for c in range(8):
        s = c % 4
        x = np.zeros((T, D), np.float32)
        x[:treal[s]] = seqs[s]
        m = dict(shared)
        m.update(cc[treal[s]])
        m["x"] = x
        in_maps.append(m)
    if "nc" not in _NC_CACHE:
        _NC_CACHE["nc"] = build_program()
    res = run_bass_kernel_spmd(_NC_CACHE["nc"], in_maps, core_ids=list(range(8)))
    ys = [np.asarray(res.results[c]["y"]) for c in range(4)]
    y_prompt = np.stack([ys[0][:treal[0]], ys[1][:treal[1]]]).astype(np.float32)
    y_sample = np.stack([ys[2][:treal[2]], ys[3][:treal[3]]]).astype(np.float32)
    return (y_prompt, y_sample)
```
